# Optimizing a Trainium2 kernel written in Bass

```python
import math
import jax, jax.numpy as jnp
from jax import lax
import numpy as np

D_MODEL = 2048
BATCH = 1
SEQ = 8192
DEPTH = 1

MLA_HEADS = 16
QK_NOPE_DIM = 128
QK_ROPE_DIM = 64
V_HEAD_DIM = 128
Q_LORA_RANK = 768
KV_LORA_RANK = 512
ROPE_THETA = 10000.0
Q_BLOCK = 128
MLA_WIDTH = MLA_HEADS * V_HEAD_DIM

SGU_GROUPS = 16
SGU_CHUNK = 128
SGU_WIDTH = D_MODEL
SGU_GROUP_DIM = SGU_WIDTH // SGU_GROUPS

NORM_EPS = 1e-6

IN_SPLITS = (Q_LORA_RANK, KV_LORA_RANK, QK_ROPE_DIM, MLA_WIDTH,
             SGU_WIDTH, SGU_WIDTH, SGU_WIDTH, D_MODEL, D_MODEL)
IN_WIDTH = sum(IN_SPLITS)

kernel_name = "hybrid_mla_sgu_gated_block"


def rmsnorm(x, g):
    xf = x.astype(jnp.float32)
    y = xf * lax.rsqrt(jnp.mean(xf * xf, axis=-1, keepdims=True) + NORM_EPS)
    return (y * g.astype(jnp.float32)).astype(x.dtype)


def split_cols(t, sizes):
    idx = list(np.cumsum(sizes)[:-1])
    return jnp.split(t, idx, axis=-1)


def rope_tables(positions, dtype):
    inv_freq = 1.0 / (ROPE_THETA ** (jnp.arange(0, QK_ROPE_DIM, 2, dtype=jnp.float32) / QK_ROPE_DIM))
    ang = positions.astype(jnp.float32)[..., None] * inv_freq
    return jnp.cos(ang).astype(dtype), jnp.sin(ang).astype(dtype)


def apply_rope(t, cos, sin):
    t1, t2 = jnp.split(t, 2, axis=-1)
    return jnp.concatenate([t1 * cos - t2 * sin, t2 * cos + t1 * sin], axis=-1)


def causal_mla_attention(q_nope, q_rope, k_nope, k_rope, v):
    B, S, H, _ = q_nope.shape
    nb = S // Q_BLOCK
    scale = 1.0 / math.sqrt(QK_NOPE_DIM + QK_ROPE_DIM)
    qn = q_nope.reshape(B, nb, Q_BLOCK, H, QK_NOPE_DIM).transpose(1, 0, 2, 3, 4)
    qr = q_rope.reshape(B, nb, Q_BLOCK, H, QK_ROPE_DIM).transpose(1, 0, 2, 3, 4)
    kpos = jnp.arange(S)

    def one_block(args):
        qn_b, qr_b, blk = args
        s = (jnp.einsum('bqhd,bkhd->bhqk', qn_b, k_nope)
             + jnp.einsum('bqhd,bkd->bhqk', qr_b, k_rope))
        s = s.astype(jnp.float32) * scale
        qpos = blk * Q_BLOCK + jnp.arange(Q_BLOCK)
        mask = kpos[None, :] <= qpos[:, None]
        s = jnp.where(mask[None, None], s, -jnp.inf)
        p = jax.nn.softmax(s, axis=-1).astype(v.dtype)
        return jnp.einsum('bhqk,bkhd->bqhd', p, v)

    o = lax.map(one_block, (qn, qr, jnp.arange(nb)))
    return o.transpose(1, 0, 2, 3, 4).reshape(B, S, H * V_HEAD_DIM)


def chunked_causal_sgu(u, v, sgu_norm_g, w_spatial, b_spatial):
    B, S, _ = v.shape
    nc = S // SGU_CHUNK
    v = rmsnorm(v, sgu_norm_g)
    vc = v.reshape(B, nc, SGU_CHUNK, SGU_GROUPS, SGU_GROUP_DIM)
    tril = jnp.tril(jnp.ones((SGU_CHUNK, SGU_CHUNK), dtype=w_spatial.dtype))
    ws = w_spatial * tril[None]
    mixed = jnp.einsum('gts,bnsgd->bntgd', ws, vc) + b_spatial.T[None, None, :, :, None]
    return u * mixed.reshape(B, S, SGU_WIDTH)


def setup_inputs(seed: int = 0) -> dict:
    key = jax.random.key(seed)
    ks = jax.random.split(key, 16)
    L, D = DEPTH, D_MODEL
    f32 = jnp.float32

    def nrm(k, shape, fan_in, mult=1.0):
        return jax.random.normal(k, shape, f32) * (mult * fan_in ** -0.5)

    def gain(k, shape):
        return 1.0 + 0.01 * jax.random.normal(k, shape, f32)

    x = jax.random.normal(ks[0], (BATCH, SEQ, D), f32)
    c = jax.random.normal(ks[1], (BATCH, D), f32)
    positions = jnp.broadcast_to(jnp.arange(SEQ, dtype=jnp.int32)[None], (BATCH, SEQ))
    return {
        "x": x,
        "c": c,
        "positions": positions,
        "attn_norm_g": gain(ks[2], (L, D)),
        "w_ada": nrm(ks[3], (L, D, 3 * D), D, 0.1),
        "b_ada": 0.01 * jax.random.normal(ks[4], (L, 3 * D), f32),
        "w_in": nrm(ks[5], (L, D, IN_WIDTH), D),
        "q_norm_g": gain(ks[6], (L, Q_LORA_RANK)),
        "w_uq": nrm(ks[7], (L, Q_LORA_RANK, MLA_HEADS * (QK_NOPE_DIM + QK_ROPE_DIM)), Q_LORA_RANK),
        "kv_norm_g": gain(ks[8], (L, KV_LORA_RANK)),
        "w_ukv": nrm(ks[9], (L, KV_LORA_RANK, MLA_HEADS * (QK_NOPE_DIM + V_HEAD_DIM)), KV_LORA_RANK),
        "sgu_norm_g": gain(ks[10], (L, SGU_WIDTH)),
        "w_spatial": nrm(ks[11], (L, SGU_GROUPS, SGU_CHUNK, SGU_CHUNK), SGU_CHUNK),
        "b_spatial": 1.0 + 0.02 * jax.random.normal(ks[12], (L, SGU_GROUPS, SGU_CHUNK), f32),
        "w_out": nrm(ks[13], (L, D, D), D),
        "final_norm_g": gain(ks[14], (D,)),
    }


def reference(x, c, positions, attn_norm_g, w_ada, b_ada, w_in, q_norm_g, w_uq, kv_norm_g,
              w_ukv, sgu_norm_g, w_spatial, b_spatial, w_out, final_norm_g):
    B, S, D = x.shape
    cos, sin = rope_tables(positions, x.dtype)
    c_act = jax.nn.silu(c)
    for l in range(DEPTH):
        mod = c_act @ w_ada[l] + b_ada[l]
        shift, scale, gate = jnp.split(mod, 3, axis=-1)
        h = rmsnorm(x, attn_norm_g[l]) * (1.0 + scale[:, None]) + shift[:, None]

        proj = h @ w_in[l]
        q_lat, kv_lat, k_rope, z_mla, u, v, z_sgu, g_mla, g_sgu = split_cols(proj, IN_SPLITS)

        q = (rmsnorm(q_lat, q_norm_g[l]) @ w_uq[l]).reshape(B, S, MLA_HEADS, QK_NOPE_DIM + QK_ROPE_DIM)
        q_nope, q_rope = q[..., :QK_NOPE_DIM], q[..., QK_NOPE_DIM:]
        q_rope = apply_rope(q_rope, cos[:, :, None, :], sin[:, :, None, :])
        kv = (rmsnorm(kv_lat, kv_norm_g[l]) @ w_ukv[l]).reshape(B, S, MLA_HEADS, QK_NOPE_DIM + V_HEAD_DIM)
        k_nope, v_mla = kv[..., :QK_NOPE_DIM], kv[..., QK_NOPE_DIM:]
        k_rope = apply_rope(k_rope, cos, sin)
        y_mla = causal_mla_attention(q_nope, q_rope, k_nope, k_rope, v_mla) * jax.nn.silu(z_mla)

        y_sgu = chunked_causal_sgu(u, v, sgu_norm_g[l], w_spatial[l], b_spatial[l]) * jax.nn.silu(z_sgu)

        merged = jax.nn.sigmoid(g_mla) * y_mla + jax.nn.sigmoid(g_sgu) * y_sgu
        x = x + gate[:, None] * (merged @ w_out[l])
    return rmsnorm(x, final_norm_g)
```

```python
import math
from contextlib import ExitStack

import numpy as np
import concourse.bass as bass
import concourse.mybir as mybir
from concourse.bass_utils import run_bass_kernel_spmd

F32 = mybir.dt.float32
BF16 = mybir.dt.bfloat16
I32 = mybir.dt.int32
AF = mybir.ActivationFunctionType
ALU = mybir.AluOpType
AX = mybir.AxisListType

NCORES = 8
D = 2048
S = 8192
TPC = S // NCORES
NB = TPC // 128
H = 16
QL, KVL, RD = 768, 512, 64
INW = 13632
EPS = 1e-6
DEBUG = {}


class Buf:
    __slots__ = ("name", "w", "r")

    def __init__(self, name):
        self.name = name
        self.w = None
        self.r = {}


class Ins:
    __slots__ = ("eng", "fn", "deps", "slot", "sig", "need_sig", "idx")

    def __init__(self, eng, fn, slot):
        self.eng = eng
        self.fn = fn
        self.slot = slot
        self.deps = set()
        self.sig = None
        self.need_sig = False


class Prog:
    ENGS = ("pe", "act", "dve", "pool", "sp")

    def __init__(self, nc):
        self.nc = nc
        self.ins = []

    def op(self, eng, fn, reads=(), writes=(), slot=None):
        I = Ins(eng, fn, slot)
        I.idx = len(self.ins)
        key = eng if slot is None else ("dma", slot)
        psr = [b for b in reads if b.name.startswith("ps")]
        if psr:
            reads = [b for b in reads if not b.name.startswith("ps")]
            writes = list(writes) + [b for b in psr if b not in writes]
        deps = set()
        for b in reads:
            Dp = b.w
            if Dp is not None and Dp is not I:
                if not (Dp.slot is None and slot is None and Dp.eng == "pe" and eng == "pe"):
                    deps.add(Dp)
        for b in writes:
            Dp = b.w
            if Dp is not None and Dp is not I:
                if not (Dp.slot is None and slot is None and Dp.eng == eng):
                    deps.add(Dp)
            for R in b.r.values():
                if R is I:
                    continue
                if R.slot is None and slot is None and R.eng == eng:
                    continue
                deps.add(R)
        for b in reads:
            b.r[key] = I
        for b in writes:
            b.w = I
            b.r = {}
        I.deps = deps
        self.ins.append(I)
        return I

    def emit(self, es):
        nc = self.nc
        for I in self.ins:
            for Dp in I.deps:
                Dp.need_sig = True
        esem = {e: es.enter_context(nc.semaphore("sem_" + e)) for e in self.ENGS}
        slots = []
        seen = set()
        for I in self.ins:
            if I.slot is not None and I.slot not in seen:
                seen.add(I.slot)
                slots.append(I.slot)
        ssem = {s: es.enter_context(nc.semaphore("dsem_%d" % i)) for i, s in enumerate(slots)}
        ecnt = {e: 0 for e in self.ENGS}
        scnt = {s: 0 for s in slots}
        for I in self.ins:
            if I.slot is not None:
                inc = 1 if I.slot == "cc" else 16
                scnt[I.slot] += inc
                I.sig = (ssem[I.slot], scnt[I.slot], inc)
            elif I.need_sig:
                ecnt[I.eng] += 1
                I.sig = (esem[I.eng], ecnt[I.eng], 1)
        per_eng = {e: [] for e in self.ENGS}
        for I in self.ins:
            per_eng[I.eng].append(I)
        final = [(ssem[s], scnt[s]) for s in slots]
        self.stats = {e: len(per_eng[e]) for e in self.ENGS}
        self.stats["nsem"] = len(slots) + len(self.ENGS)
        self.stats["sig"] = dict(ecnt)

        def run_engine(ename, handle):
            waited = {}
            for I in per_eng[ename]:
                need = {}
                for Dp in I.deps:
                    sem, val, _ = Dp.sig
                    k = id(sem)
                    if waited.get(k, 0) >= val:
                        continue
                    if k not in need or need[k][1] < val:
                        need[k] = (sem, val)
                for k, (sem, val) in need.items():
                    handle.wait_ge(sem, val)
                    waited[k] = val
                bi = I.fn(handle)
                if I.sig is not None:
                    bi.then_inc(I.sig[0], I.sig[2])
            if ename == "sp":
                for sem, val in final:
                    if val > 0:
                        handle.wait_ge(sem, val)

        with nc.Block() as block:
            @block.tensor
            def _(h):
                run_engine("pe", h)

            @block.scalar
            def _(h):
                run_engine("act", h)

            @block.vector
            def _(h):
                run_engine("dve", h)

            @block.gpsimd
            def _(h):
                run_engine("pool", h)

            @block.sync
            def _(h):
                run_engine("sp", h)


PI = math.pi
QSCALE = 1.0 / math.sqrt(192.0)
GW = 640
OFF_Q, OFF_KV, OFF_KR, OFF_ZM, OFF_U, OFF_V, OFF_ZS, OFF_GM, OFF_GS = (
    0, 768, 1280, 1344, 3392, 5440, 7488, 9536, 11584)


class Region:
    def __init__(self, t, ncols):
        self.t = t
        self.n = ncols
        self.off = 0

    def reset(self, off=0):
        self.off = off

    def f32(self, ncols):
        assert self.off + ncols <= self.n, (self.off, ncols, self.n)
        ap = self.t[:, self.off:self.off + ncols]
        self.off += ncols
        return ap

    def bf16(self, nelem):
        nc32 = (nelem + 1) // 2
        return self.f32(nc32).bitcast(BF16)


class _Stop(Exception):
    pass


def build_program(dbg=()):
    STOP = DEBUG.get("stop", 99)
    nc = bass.Bass("TRN2", target_bir_lowering=False)
    P = Prog(nc)

    def din(name, shape, dt=F32):
        return nc.dram_tensor(name, list(shape), dt, kind="ExternalInput")

    x_d = din("x", [TPC, D])
    pos_d = din("pos", [128, NB], I32)
    c_d = din("cT", [128, 16])
    wada_d = din("w_ada", [D, 3 * D])
    bada_d = din("b_ada", [1, 3 * D])
    gattn_d = din("attn_norm_g", [1, D])
    win_d = din("w_in", [D, INW])
    qg_d = din("q_norm_g", [1, QL])
    kvg_d = din("kv_norm_g", [1, KVL])
    wuq_d = din("w_uq", [H, QL, 384])
    wukv_d = din("w_ukv", [H, KVL, 256])
    gv_d = din("sgu_norm_g", [1, D])
    wsT_d = din("wsT", [128, 16, 128])
    bspT_d = din("bspT", [128, 16])
    wout_d = din("w_out", [D, D])
    fng_d = din("final_norm_g", [1, D])
    mask_d = din("mask", [128, 8, 128])
    tri_d = din("tri", [128, 128])
    ident_d = din("ident", [128, 128])
    invf_d = din("invf", [128, 32])
    out_d = nc.dram_tensor("out", [TPC, D], F32, kind="ExternalOutput")
    mod_d = nc.dram_tensor("mod_scr", [1, 3 * D], F32)
    gm_d = nc.dram_tensor("gm_scr", [H, 128, NB, 128], F32)
    ysg_d = nc.dram_tensor("ysg_scr", [H, 128, NB, 128], F32)
    agin_d = nc.dram_tensor("ag_in", [576, TPC], BF16)
    agout_d = nc.dram_tensor("ag_out", [NCORES * 576, TPC], BF16)
    dbg_out = {}
    for name, shape, dt in dbg:
        dbg_out[name] = nc.dram_tensor("dbg_" + name, list(shape), dt, kind="ExternalOutput")

    es = ExitStack()
    with es:
        try:
            def sbt(name, ncols):
                return es.enter_context(nc.sbuf_tensor(name, [128, ncols], F32))

            R1 = Region(sbt("R1", 20480), 20480)
            R2 = Region(sbt("R2", 8192), 8192)
            R3 = Region(sbt("R3", 12800), 12800)
            R5 = Region(sbt("R5", 3072), 3072)
            R6 = Region(sbt("R6", 4096), 4096)
            R7 = Region(sbt("R7", 3584), 3584)
            psum = es.enter_context(nc.psum_tensor("ps", [128, 4096], F32))
            PS = [psum[:, k * 512:(k + 1) * 512] for k in range(8)]
            PSb = [Buf("ps%d" % k) for k in range(8)]
            ps_rr = [0]

            def next_bank(banks=range(8)):
                banks = list(banks)
                k = banks[ps_rr[0] % len(banks)]
                ps_rr[0] += 1
                return k

            def fence(olds, news, eng="pool"):
                olds = list(olds)
                P.op(eng, lambda e: e.memset(fscr[:, 0:1], 0.0), reads=olds, writes=olds + list(news) + [fscr_b])

            def dma(out, in_, reads, writes, slot, eng="sp"):
                return P.op(eng, lambda e: e.dma_start(out=out, in_=in_), reads=reads, writes=writes, slot=slot)

            def mm(out, lhsT, rhs, start, stop, reads, writes, skip=False):
                if skip:
                    return P.op("pe", lambda e: e.matmul(out, lhsT, rhs, start=start, stop=stop,
                                                         skip_group_check=True), reads=reads, writes=writes)
                return P.op("pe", lambda e: e.matmul(out, lhsT, rhs, start=start, stop=stop),
                            reads=reads, writes=writes)

            def tr(out, in_, ident, reads, writes):
                return P.op("pe", lambda e: e.transpose(out, in_, ident), reads=reads, writes=writes)

            def act(out, in_, func, reads, writes, **kw):
                return P.op("act", lambda e: e.activation(out=out, in_=in_, func=func, **kw), reads=reads, writes=writes)

            def dve(name, reads, writes, *a, **kw):
                return P.op("dve", lambda e: getattr(e, name)(*a, **kw), reads=reads, writes=writes)

            cp_rr = [0]

            def copy_any(out, in_, reads, writes, scale=None):
                cp_rr[0] += 1
                if scale is not None or cp_rr[0] % 2 == 0:
                    kw = {} if scale is None else {"scale": scale}
                    return act(out, in_, AF.Copy, reads, writes, **kw)
                return dve("tensor_copy", reads, writes, out=out, in_=in_)

            def rstd_from_ssq(st_ap, buf, n):
                act(st_ap, st_ap, AF.Sqrt, [buf], [buf], scale=1.0 / n, bias=EPS)
                dve("reciprocal", [buf], [buf], out=st_ap, in_=st_ap)

            ident_f = R7.f32(128)
            ident_b = R7.bf16(128)
            tri_b = R7.bf16(128)
            maskb = R7.bf16(1024).rearrange("p (r q) -> p r q", r=8)
            bspT = R7.f32(16)
            invf = R7.f32(32)
            pos_i = R7.f32(8).bitcast(I32)
            pos_f = R7.f32(8)
            cos_t = R7.f32(256).rearrange("p (b d) -> p b d", b=8)
            sin_t = R7.f32(256).rearrange("p (b d) -> p b d", b=8)
            cs1T = R7.f32(1024)
            cs2T = R7.f32(1024)
            cT = R7.f32(16)
            csig = R7.f32(16)
            cact = R7.f32(16)
            st_x = R7.f32(8)
            st_q = R7.f32(8)
            st_kv = R7.f32(8)
            st_v4 = R7.f32(32).rearrange("p (b n) -> p b n", b=8)
            st_v = R7.f32(8)
            st_f4 = R7.f32(32).rearrange("p (b n) -> p b n", b=8)
            st_f = R7.f32(8)
            rinv = R7.f32(16)
            fscr = R7.f32(4)
            fscr_b = Buf("fscr")
            b_ident_f, b_ident_b, b_tri, b_mask, b_bsp, b_invf = (Buf(n) for n in
                                                                  ("identf", "identb", "tri", "mask", "bsp", "invf"))
            b_pos, b_cos, b_sin, b_cs1, b_cs2, b_c = (Buf(n) for n in ("pos", "cos", "sin", "cs1", "cs2", "c"))
            b_stx = [Buf("stx%d" % b) for b in range(NB)]
            b_stq = [Buf("stq%d" % b) for b in range(NB)]
            b_stkv = [Buf("stkv%d" % b) for b in range(NB)]
            b_stv = [Buf("stv%d" % b) for b in range(NB)]
            b_stf = [Buf("stf%d" % b) for b in range(NB)]

            R1.reset()
            mask_st = R1.f32(1024).rearrange("p (r q) -> p r q", r=8)
            tri_st = R1.f32(128)
            b_tmp0 = Buf("tmp0")
            dma(ident_f, ident_d[:, :], [], [b_ident_f], "c0")
            dma(mask_st, mask_d[:, :, :], [], [b_tmp0], "c1")
            dma(tri_st, tri_d[:, :], [], [b_tmp0], "c1")
            dma(bspT, bspT_d[:, :], [], [b_bsp], "c2")
            dma(invf, invf_d[:, :], [], [b_invf], "c2")
            dma(pos_i, pos_d[:, :], [], [b_pos], "c2")
            dma(cT, c_d[:, :], [], [b_c], "c2")
            dve("tensor_copy", [b_ident_f], [b_ident_b], out=ident_b, in_=ident_f)
            dve("tensor_copy", [b_tmp0], [b_mask], out=maskb, in_=mask_st)
            dve("tensor_copy", [b_tmp0], [b_tri], out=tri_b, in_=tri_st)
            dve("tensor_copy", [b_pos], [b_pos], out=pos_f, in_=pos_i)

            act(csig, cT, AF.Sigmoid, [b_c], [b_c])
            dve("tensor_tensor", [b_c], [b_c], out=cact, in0=cT, in1=csig, op=ALU.mult)
            R1.reset(1152)
            wa_st = [R1.f32(8192).rearrange("p (k n) -> p k n", k=16) for _ in range(2)]
            b_wa = [Buf("wa%d" % i) for i in range(2)]
            R2.reset()
            modrow = R2.f32(8192)[0:1, 0:6144]
            b_modrow = Buf("modrow")
            R3.reset()
            badarow = R3.f32(6144)[0:1, :]
            b_bada = Buf("bada")
            dma(badarow, bada_d[0:1, :], [], [b_bada], "c3")
            wada_v = wada_d.ap().rearrange("(k p) n -> p k n", p=128)
            for n in range(12):
                s_ = n % 2
                for hh in range(2):
                    dma(wa_st[s_][:, hh * 8:(hh + 1) * 8, :], wada_v[:, hh * 8:(hh + 1) * 8, n * 512:(n + 1) * 512],
                        [], [b_wa[s_]], "wa%d" % s_)
                bk = next_bank()
                for k in range(16):
                    mm(PS[bk][0:1, :], cact[:, k:k + 1], wa_st[s_][:, k, :], k == 0, k == 15,
                       [b_c, b_wa[s_]], [PSb[bk]])
                dve("tensor_tensor", [PSb[bk], b_bada], [b_modrow], out=modrow[:, n * 512:(n + 1) * 512],
                    in0=PS[bk][0:1, :], in1=badarow[:, n * 512:(n + 1) * 512], op=ALU.add)
            b_modd = Buf("mod_d")
            dma(mod_d[0:1, :], modrow, [b_modrow], [b_modd], "c4")

            R3.reset(6144)
            ang = R3.f32(256).rearrange("p (b d) -> p b d", b=8)
            sarg = R3.f32(256)
            carg = R3.f32(256)
            T1 = R3.f32(1024).rearrange("p (b d) -> p b d", b=8)
            T2 = R3.f32(1024).rearrange("p (b d) -> p b d", b=8)
            b_ang, b_T = Buf("ang"), Buf("T12")
            for b in range(NB):
                dve("tensor_scalar", [b_invf, b_pos], [b_ang], out=ang[:, b, :], in0=invf, scalar1=pos_f[:, b:b + 1],
                    scalar2=None, op0=ALU.mult)
            angf = ang.rearrange("p b d -> p (b d)")
            MAGIC = 12582912.0
            C1 = 6.28125
            C2 = 2.0 * PI - 6.28125
            nfs = R3.f32(256)
            for (dst, dbuf, r_ap, phase) in ((sin_t, b_sin, sarg, 0.0), (cos_t, b_cos, carg, 0.25)):
                dve("tensor_scalar", [b_ang], [b_ang], out=nfs, in0=angf, scalar1=1.0 / (2.0 * PI), scalar2=phase,
                    op0=ALU.mult, op1=ALU.add)
                dve("tensor_scalar", [b_ang], [b_ang], out=nfs, in0=nfs, scalar1=MAGIC, scalar2=None, op0=ALU.add)
                dve("tensor_scalar", [b_ang], [b_ang], out=nfs, in0=nfs, scalar1=-MAGIC, scalar2=None, op0=ALU.add)
                dve("scalar_tensor_tensor", [b_ang], [b_ang], out=r_ap, in0=nfs, scalar=-C1, in1=angf,
                    op0=ALU.mult, op1=ALU.add)
                dve("scalar_tensor_tensor", [b_ang], [b_ang], out=r_ap, in0=nfs, scalar=-C2, in1=r_ap,
                    op0=ALU.mult, op1=ALU.add)
                dve("tensor_scalar", [b_ang], [b_ang], out=r_ap, in0=r_ap, scalar1=phase * 2.0 * PI, scalar2=-PI,
                    op0=ALU.add, op1=ALU.max)
                dve("tensor_scalar", [b_ang], [b_ang], out=r_ap, in0=r_ap, scalar1=PI, scalar2=None, op0=ALU.min)
                act(dst.rearrange("p b d -> p (b d)"), r_ap, AF.Sin, [b_ang], [dbuf])
            for rr in range(4):
                dve("tensor_copy", [b_cos], [b_T], out=T1[:, :, rr * 32:(rr + 1) * 32], in_=cos_t)
                if rr % 2 == 0:
                    dve("tensor_scalar", [b_sin], [b_T], out=T2[:, :, rr * 32:(rr + 1) * 32], in0=sin_t, scalar1=-1.0,
                        scalar2=None, op0=ALU.mult)
                else:
                    dve("tensor_copy", [b_sin], [b_T], out=T2[:, :, rr * 32:(rr + 1) * 32], in_=sin_t)
            for (Tt, dst, bd) in ((T1, cs1T, b_cs1), (T2, cs2T, b_cs2)):
                for half in range(2):
                    bk = next_bank()
                    for bb in range(4):
                        b = half * 4 + bb
                        tr(PS[bk][:, bb * 128:(bb + 1) * 128], Tt[:, b, :], ident_f, [b_T, b_ident_f], [PSb[bk]])
                    act(dst[:, half * 512:(half + 1) * 512], PS[bk][:, :], AF.Copy, [PSb[bk]], [bd], scale=QSCALE)

            R1.reset()
            x_st = [R1.f32(2048) for _ in range(2)]
            b_xst = [Buf("xst%d" % i) for i in range(2)]
            hf = R1.f32(2048)
            hb = R1.bf16(2048)
            junk = R1.f32(2048)
            A_bc = R1.f32(2048)
            shift_bc = R1.f32(2048)
            gattn_bc = R1.f32(2048)
            b_hf, b_hb, b_junk, b_A, b_shift, b_gattn = (Buf(n) for n in ("hf", "hb", "junk", "A", "shift", "gattn"))
            fence([b_wa[0], b_wa[1], b_tmp0], [b_xst[0], b_xst[1], b_hf, b_hb, b_junk, b_A, b_shift, b_gattn])
            dma(A_bc, mod_d[0:1, D:2 * D].broadcast_to([128, D]), [b_modd], [b_A], "c5")
            dma(shift_bc, mod_d[0:1, 0:D].broadcast_to([128, D]), [b_modd], [b_shift], "c5")
            dma(gattn_bc, gattn_d[0:1, :].broadcast_to([128, D]), [], [b_gattn], "c5")
            dve("scalar_tensor_tensor", [b_A, b_gattn], [b_A], out=A_bc, in0=A_bc, scalar=1.0, in1=gattn_bc,
                op0=ALU.add, op1=ALU.mult)
            hT = R2.t[:, 0:8192].bitcast(BF16).rearrange("p (k t) -> p k t", k=16)
            b_hT = [Buf("hT%d" % b) for b in range(NB)]
            fence([b_modrow], b_hT)
            for b in range(NB):
                s_ = b % 2
                dma(x_st[s_], x_d[b * 128:(b + 1) * 128, :], [], [b_xst[s_]], "xst%d" % s_)
                act(junk, x_st[s_], AF.Square, [b_xst[s_]], [b_junk, b_stx[b]], accum_out=st_x[:, b:b + 1])
                rstd_from_ssq(st_x[:, b:b + 1], b_stx[b], D)
                dve("scalar_tensor_tensor", [b_xst[s_], b_stx[b], b_A], [b_hf], out=hf, in0=x_st[s_],
                    scalar=st_x[:, b:b + 1], in1=A_bc, op0=ALU.mult, op1=ALU.mult)
                dve("tensor_tensor", [b_hf, b_shift], [b_hb], out=hb, in0=hf, in1=shift_bc, op=ALU.add)
                for half in range(2):
                    bk = next_bank()
                    pb = PS[bk].bitcast(BF16)
                    for kk in range(8):
                        k = half * 8 + kk
                        tr(pb[:, kk * 128:(kk + 1) * 128], hb[:, k * 128:(k + 1) * 128], ident_b,
                           [b_hb, b_ident_b], [PSb[bk]])
                    copy_any(hT[:, half * 8:(half + 1) * 8, b * 128:(b + 1) * 128],
                             pb.rearrange("p (k t) -> p k t", k=8), [PSb[bk]], [b_hT[b]])

            if STOP < 4:
                raise _Stop()
            R3.reset()
            NST = 2
            w_st = [R3.f32(2 * GW).rearrange("p (k n) -> p k n", k=2) for _ in range(NST)]
            b_wst = [Buf("wst%d" % i) for i in range(NST)]
            w_bf = [R3.bf16(16 * GW).rearrange("p (k n) -> p k n", k=16) for _ in range(2)]
            b_wbf = [[Buf("wbf%d_%d" % (i, q)) for q in range(8)] for i in range(2)]
            fence([b_bada, b_ang, b_T], b_wst + [q for l in b_wbf for q in l])
            wst_rr = [0]
            nst_active = [NST]
            w_st_x = [R6.t[:, 1024 + i * 2 * GW:1024 + (i + 1) * 2 * GW].rearrange("p (k n) -> p k n", k=2)
                      for i in range(2)]
            b_wst_x = [Buf("wst%d" % (NST + i)) for i in range(2)]
            wbf_rr = [0]

            def load_weight_cg(src, pieces):
                slot = wbf_rr[0] % 2
                wbf_rr[0] += 1
                wtot = sum(w for _, w in pieces)
                v = src.ap().rearrange("(k p) n -> p k n", p=128)
                for q in range(8):
                    ss = wst_rr[0] % nst_active[0]
                    wst_rr[0] += 1
                    c = 0
                    for (c0, w) in pieces:
                        dma(w_st[ss][:, :, c:c + w], v[:, 2 * q:2 * q + 2, c0:c0 + w], [], [b_wst[ss]], "wst%d" % ss)
                        c += w
                    copy_any(w_bf[slot][:, 2 * q:2 * q + 2, 0:wtot], w_st[ss][:, :, 0:wtot], [b_wst[ss]],
                             [b_wbf[slot][q]])
                return w_bf[slot], b_wbf[slot]

            R1.reset()
            lat = R1.f32(NB * 1344).rearrange("p (b n) -> p b n", b=NB)
            b_lat = [Buf("lat%d" % b) for b in range(NB)]
            v_bf = R1.bf16(NB * 2048).rearrange("p (b n) -> p b n", b=NB)
            b_vbf = [Buf("vbf%d" % b) for b in range(NB)]
            qg_bc = R1.f32(768)
            kvg_bc = R1.f32(512)
            b_qg, b_kvg = Buf("qg"), Buf("kvg")
            fence([b_xst[0], b_xst[1], b_hf, b_hb, b_junk, b_A, b_shift, b_gattn], b_lat + b_vbf + [b_qg, b_kvg])
            dma(qg_bc, qg_d[0:1, :].broadcast_to([128, QL]), [], [b_qg], "c6")
            dma(kvg_bc, kvg_d[0:1, :].broadcast_to([128, KVL]), [], [b_kvg], "c6")
            qnT = R5.t[:, 0:3072].bitcast(BF16).rearrange("p (k t) -> p k t", k=6)
            b_qnT = [Buf("qnT%d" % b) for b in range(NB)]
            R6.reset()
            kvnT_own = R6.bf16(4 * 1024).rearrange("p (k t) -> p k t", k=4)
            kropeT_own = R6.bf16(1024)
            b_kvown = [Buf("kvown%d" % b) for b in range(NB)]
            qn_b = R6.bf16(768)
            kvn_b = R6.bf16(512)
            krot = R6.bf16(64)
            rt = [R6.f32(32) for _ in range(4)]
            sqj = R6.bf16(768)
            b_qnb, b_kvnb, b_krot, b_rt, b_sqj = Buf("qnb"), Buf("kvnb"), Buf("krot"), Buf("rt"), Buf("sqj")

            def inproj_cg(wt, wb, width, evac):
                for b in range(NB):
                    bk = next_bank()
                    for k in range(16):
                        mm(PS[bk][:, 0:width], hT[:, k, b * 128:(b + 1) * 128], wt[:, k, 0:width], k == 0, k == 15,
                           [b_hT[b], wb[k // 2]], [PSb[bk]])
                    evac(b, bk)

            for (c0, w) in ((0, 512), (512, 512), (1024, 320)):
                wt, wb = load_weight_cg(win_d, [(c0, w)])

                def evac_lat(b, bk, c0=c0, w=w):
                    copy_any(lat[:, b, c0:c0 + w], PS[bk][:, 0:w], [PSb[bk]], [b_lat[b]])
                inproj_cg(wt, wb, w, evac_lat)

            def lat_post(b):
                act(sqj, lat[:, b, 0:768], AF.Square, [b_lat[b]], [b_sqj, b_stq[b]], accum_out=st_q[:, b:b + 1])
                rstd_from_ssq(st_q[:, b:b + 1], b_stq[b], QL)
                dve("scalar_tensor_tensor", [b_lat[b], b_stq[b], b_qg], [b_qnb], out=qn_b, in0=lat[:, b, 0:768],
                    scalar=st_q[:, b:b + 1], in1=qg_bc, op0=ALU.mult, op1=ALU.mult)
                bk = next_bank()
                pb = PS[bk].bitcast(BF16)
                for k in range(6):
                    tr(pb[:, k * 128:(k + 1) * 128], qn_b[:, k * 128:(k + 1) * 128], ident_b, [b_qnb, b_ident_b],
                       [PSb[bk]])
                copy_any(qnT[:, :, b * 128:(b + 1) * 128], pb[:, 0:768].rearrange("p (k t) -> p k t", k=6),
                         [PSb[bk]], [b_qnT[b]])
                act(sqj[:, 0:512], lat[:, b, 768:1280], AF.Square, [b_lat[b]], [b_sqj, b_stkv[b]],
                    accum_out=st_kv[:, b:b + 1])
                rstd_from_ssq(st_kv[:, b:b + 1], b_stkv[b], KVL)
                dve("scalar_tensor_tensor", [b_lat[b], b_stkv[b], b_kvg], [b_kvnb], out=kvn_b, in0=lat[:, b, 768:1280],
                    scalar=st_kv[:, b:b + 1], in1=kvg_bc, op0=ALU.mult, op1=ALU.mult)
                t1 = lat[:, b, 1280:1312]
                t2 = lat[:, b, 1312:1344]
                dve("tensor_tensor", [b_lat[b], b_cos], [b_rt], out=rt[0], in0=t1, in1=cos_t[:, b, :], op=ALU.mult)
                dve("tensor_tensor", [b_lat[b], b_sin], [b_rt], out=rt[1], in0=t2, in1=sin_t[:, b, :], op=ALU.mult)
                dve("tensor_tensor", [b_lat[b], b_cos], [b_rt], out=rt[2], in0=t2, in1=cos_t[:, b, :], op=ALU.mult)
                dve("tensor_tensor", [b_lat[b], b_sin], [b_rt], out=rt[3], in0=t1, in1=sin_t[:, b, :], op=ALU.mult)
                dve("tensor_tensor", [b_rt], [b_krot], out=krot[:, 0:32], in0=rt[0], in1=rt[1], op=ALU.subtract)
                dve("tensor_tensor", [b_rt], [b_krot], out=krot[:, 32:64], in0=rt[2], in1=rt[3], op=ALU.add)
                bk = next_bank()
                pb = PS[bk].bitcast(BF16)
                for k in range(4):
                    tr(pb[:, k * 128:(k + 1) * 128], kvn_b[:, k * 128:(k + 1) * 128], ident_b, [b_kvnb, b_ident_b],
                       [PSb[bk]])
                tr(pb[0:64, 512:640], krot[:, 0:64], ident_b, [b_krot, b_ident_b], [PSb[bk]])
                copy_any(kvnT_own[:, :, b * 128:(b + 1) * 128], pb[:, 0:512].rearrange("p (k t) -> p k t", k=4),
                         [PSb[bk]], [b_kvown[b]])
                copy_any(kropeT_own[0:64, b * 128:(b + 1) * 128], pb[0:64, 512:640], [PSb[bk]], [b_kvown[b]])

            def evac_v_factory(n):
                def evac_v(b, bk):
                    act(sqj[:, 0:512], PS[bk][:, 0:512], AF.Square, [PSb[bk]], [b_sqj, b_stv[b]],
                        accum_out=st_v4[:, b, n:n + 1])
                    dve("tensor_copy", [PSb[bk]], [b_vbf[b]], out=v_bf[:, b, n * 512:(n + 1) * 512], in_=PS[bk][:, 0:512])
                return evac_v

            for n in range(4):
                wt, wb = load_weight_cg(win_d, [(1344 + n * 512, 512)])
                inproj_cg(wt, wb, 512, evac_v_factory(n))
                if n < 2:
                    for b in range(n * 4, n * 4 + 4):
                        lat_post(b)
                if n == 1:
                    b_agin, b_agout = Buf("agin"), Buf("agout")
                    dma(agin_d[0:512, :].rearrange("(k p) t -> p k t", p=128), kvnT_own, b_kvown, [b_agin], "c7")
                    dma(agin_d[512:576, :], kropeT_own[0:64, :], b_kvown, [b_agin], "c7")
                    P.op("pool", lambda e: e.collective_compute("AllGather", ALU.bypass,
                                                                replica_groups=[list(range(NCORES))],
                                                                ins=[agin_d.ap().opt()], outs=[agout_d.ap().opt()]),
                         reads=[b_agin], writes=[b_agout], slot="cc")
            if "qnT" in dbg_out:
                dma(dbg_out["qnT"][:, :, :], qnT, b_qnT, [], "dbg")
            if "agout" in dbg_out:
                dma(dbg_out["agout"][:, :], agout_d[:, :], [b_agout], [], "dbg")

            if STOP < 5:
                raise _Stop()
            R1.reset()
            wsT_st = R1.f32(2048).rearrange("p (g t) -> p g t", g=16)
            gv_bc = R1.f32(2048)
            NT = 2
            sgs = [R1.f32(256) for _ in range(NT)]
            sgm = [R1.f32(256) for _ in range(NT)]
            ub = [R1.f32(128) for _ in range(NT)]
            tt1 = [R1.f32(128) for _ in range(NT)]
            tt2 = [R1.f32(128) for _ in range(NT)]
            tm = [R1.f32(128) for _ in range(NT)]
            ta = [R1.f32(128) for _ in range(NT)]
            gm_st = [R1.f32(1024).rearrange("p (b n) -> p b n", b=NB) for _ in range(2)]
            ysg_st = [R1.f32(1024).rearrange("p (b n) -> p b n", b=NB) for _ in range(2)]
            b_wsst, b_gv = Buf("wsst"), Buf("gv")
            b_t = [Buf("dtmp%d" % i) for i in range(NT)]
            b_gmst = [Buf("gmst%d" % i) for i in range(2)]
            b_ysgst = [Buf("ysgst%d" % i) for i in range(2)]
            fence(b_lat, [b_wsst, b_gv] + b_t + b_gmst + b_ysgst)
            wsT_b = R6.t[:, 0:1024].bitcast(BF16).rearrange("p (g t) -> p g t", g=16)
            b_ws = Buf("wsTb")
            fence(b_kvown + [b_qnb, b_kvnb, b_krot, b_rt], [b_ws])
            fence(b_kvown + [b_qnb, b_kvnb, b_krot, b_rt, b_sqj], b_wst_x)
            w_st = w_st + w_st_x
            b_wst = b_wst + b_wst_x
            nst_active[0] = NST + 2
            dma(wsT_st, wsT_d[:, :, :], [], [b_wsst], "c8")
            dma(gv_bc, gv_d[0:1, :].broadcast_to([128, D]), [], [b_gv], "c8")
            for g in range(16):
                dve("tensor_tensor", [b_wsst, b_tri], [b_ws], out=wsT_b[:, g, :], in0=wsT_st[:, g, :], in1=tri_b,
                    op=ALU.mult)
            for b in range(NB):
                dve("tensor_reduce", [b_stv[b]], [b_stv[b]], out=st_v[:, b:b + 1], in_=st_v4[:, b, :], axis=AX.X,
                    op=ALU.add)
                rstd_from_ssq(st_v[:, b:b + 1], b_stv[b], D)
                dve("scalar_tensor_tensor", [b_vbf[b], b_stv[b], b_gv], [b_vbf[b]], out=v_bf[:, b, :], in0=v_bf[:, b, :],
                    scalar=st_v[:, b:b + 1], in1=gv_bc, op0=ALU.mult, op1=ALU.mult)
            b_gmd = [Buf("gmd%d" % g) for g in range(H)]
            b_ysgd = [Buf("ysgd%d" % g) for g in range(H)]
            tcnt = [0]
            for g in range(H):
                o = g * 128
                wt, wb = load_weight_cg(win_d, [(3392 + g * GW, GW)])
                so = g % 2
                for b in range(NB):
                    bx = next_bank()
                    by = next_bank()
                    for k in range(16):
                        mm(PS[bx][:, 0:384], hT[:, k, b * 128:(b + 1) * 128], wt[:, k, 0:384], k == 0, k == 15,
                           [b_hT[b], wb[k // 2]], [PSb[bx]])
                        mm(PS[by][:, 0:256], hT[:, k, b * 128:(b + 1) * 128], wt[:, k, 384:640], k == 0, k == 15,
                           [b_hT[b], wb[k // 2]], [PSb[by]])
                    mm(PS[by][:, 256:384], wsT_b[:, g, :], v_bf[:, b, o:o + 128], True, True, [b_ws, b_vbf[b]], [PSb[by]])
                    ti = tcnt[0] % NT
                    tcnt[0] += 1
                    bt = b_t[ti]
                    act(sgs[ti], PS[bx][:, 128:384], AF.Sigmoid, [PSb[bx]], [bt])
                    act(sgm[ti], PS[by][:, 0:256], AF.Sigmoid, [PSb[by]], [bt])
                    act(ub[ti], PS[bx][:, 0:128], AF.Copy, [PSb[bx]], [bt])
                    dve("tensor_tensor", [PSb[bx], bt], [bt], out=tt1[ti], in0=PS[bx][:, 128:256], in1=sgs[ti][:, 0:128],
                        op=ALU.mult)
                    dve("tensor_tensor", [bt], [bt], out=tt2[ti], in0=tt1[ti], in1=sgs[ti][:, 128:256], op=ALU.mult)
                    dve("scalar_tensor_tensor", [PSb[by], bt, b_bsp], [bt], out=tm[ti], in0=PS[by][:, 256:384],
                        scalar=bspT[:, g:g + 1], in1=ub[ti], op0=ALU.add, op1=ALU.mult)
                    dve("tensor_tensor", [bt], [b_ysgst[so]], out=ysg_st[so][:, b, :], in0=tm[ti], in1=tt2[ti],
                        op=ALU.mult)
                    dve("tensor_tensor", [PSb[by], bt], [bt], out=ta[ti], in0=PS[by][:, 0:128], in1=sgm[ti][:, 0:128],
                        op=ALU.mult)
                    dve("tensor_tensor", [bt], [b_gmst[so]], out=gm_st[so][:, b, :], in0=ta[ti], in1=sgm[ti][:, 128:256],
                        op=ALU.mult)
                dma(gm_d[g], gm_st[so], [b_gmst[so]], [b_gmd[g]], "gmst%d" % so)
                dma(ysg_d[g], ysg_st[so], [b_ysgst[so]], [b_ysgd[g]], "ysgst%d" % so)
            if "gm" in dbg_out:
                dma(dbg_out["gm"][:, :, :, :], gm_d[:, :, :, :], b_gmd, [], "dbg")
                dma(dbg_out["ysg"][:, :, :, :], ysg_d[:, :, :, :], b_ysgd, [], "dbg")

            if STOP < 6:
                raise _Stop()
            kvnT_all = R1.t[:, 0:16384].bitcast(BF16).rearrange("p (c r j t) -> p c r j t", c=4, r=8, j=8)
            kropeT_all = R1.t[:, 16384:20480].bitcast(BF16).rearrange("p (r j t) -> p r j t", r=8, j=8)
            b_kvall = [Buf("kvall%d" % r) for r in range(NCORES)]
            fence(b_vbf + [b_wsst, b_gv, b_qg, b_kvg] + b_t + b_gmst + b_ysgst, b_kvall)
            P.op("dve", lambda e: e.memset(kropeT_all[64:128, :, :, :], 0.0), reads=[], writes=b_kvall)
            for r in range(NCORES):
                dma(kvnT_all[:, :, r, :, :].rearrange("p c j t -> p c (j t)"),
                    agout_d[r * 576:r * 576 + 512, :].rearrange("(c p) t -> p c t", p=128), [b_agout], [b_kvall[r]],
                    "kvall")
                dma(kropeT_all[0:64, r, :, :].rearrange("p j t -> p (j t)"), agout_d[r * 576 + 512:(r + 1) * 576, :],
                    [b_agout], [b_kvall[r]], "kvall")

            mergedT = R2.t[:, 0:8192].bitcast(BF16).rearrange("p (k t) -> p k t", k=16)
            b_mT = [Buf("mT%d" % b) for b in range(NB)]
            fence(b_hT, b_mT)
            R3.reset()
            wq_st = R3.f32(6 * 384).rearrange("p (k n) -> p k n", k=6)
            wkv_st = R3.f32(4 * 256).rearrange("p (k n) -> p k n", k=4)
            wq_b = [R3.bf16(6 * 384).rearrange("p (k n) -> p k n", k=6) for _ in range(2)]
            wkv_b = [R3.bf16(4 * 256).rearrange("p (k n) -> p k n", k=4) for _ in range(2)]
            qT = [R3.bf16(1024) for _ in range(2)]
            qrT = [R3.bf16(1024) for _ in range(2)]
            gmh = [R3.f32(1024).rearrange("p (b n) -> p b n", b=NB) for _ in range(2)]
            ysgh = [R3.f32(1024).rearrange("p (b n) -> p b n", b=NB) for _ in range(2)]
            R6.reset()
            KT = [R6.bf16(1024).rearrange("p (r t) -> p r t", r=8) for _ in range(2)]
            Vb = [R6.bf16(8 * 130).rearrange("p (r n) -> p r n", r=8) for _ in range(2)]
            PT = [R6.bf16(512) for _ in range(3)]
            mtmp = [R6.f32(128) for _ in range(2)]
            mrgb = [R6.bf16(128) for _ in range(2)]
            ropeA = R6.f32(512)
            osb = [R6.f32(130) for _ in range(2)]
            b_wqst, b_wkvst = Buf("wqst"), Buf("wkvst")
            b_wq = [Buf("wq%d" % i) for i in range(2)]
            b_wkv = [Buf("wkv%d" % i) for i in range(2)]
            b_qT = [Buf("qT%d" % i) for i in range(2)]
            b_gmh = [Buf("gmh%d" % i) for i in range(2)]
            b_ysgh = [Buf("ysgh%d" % i) for i in range(2)]
            b_KT = [Buf("KT%d" % i) for i in range(2)]
            b_V = [Buf("V%d" % i) for i in range(2)]
            b_PT = [Buf("PT%d" % i) for i in range(3)]
            b_mtmp = [Buf("mtmp%d" % i) for i in range(2)]
            b_ropeA = Buf("ropeA")
            b_O = [Buf("O%d" % i) for i in range(NB)]
            fence(b_wst + [q for l in b_wbf for q in l],
                  [b_wqst, b_wkvst] + b_wq + b_wkv + b_qT + b_gmh + b_ysgh)
            fence([b_ws] + b_wst_x, b_KT + b_V + b_PT + b_mtmp + [b_ropeA])
            for s_ in range(2):
                P.op("dve", lambda e, s_=s_: e.memset(Vb[s_][:, :, 128:130], 1.0), reads=[], writes=[b_V[s_]])
            OB = [0, 0, 0, 1, 1, 1, 2, 2]
            OC = [0, 129, 258, 0, 129, 258, 0, 129]

            def Oap(qb, lo, hi):
                return PS[OB[qb]][:, OC[qb] + lo:OC[qb] + hi]
            st_rr = [0]
            gen_rr = [0]

            def gen_bank():
                gen_rr[0] += 1
                return 6 + gen_rr[0] % 2

            def head_prologue(h):
                s_ = h % 2
                dma(wq_st, wuq_d[h].rearrange("(k p) n -> p k n", p=128), [], [b_wqst], "wqst")
                dma(wkv_st, wukv_d[h].rearrange("(k p) n -> p k n", p=128), [], [b_wkvst], "wkvst")
                copy_any(wq_b[s_], wq_st, [b_wqst], [b_wq[s_]])
                copy_any(wkv_b[s_], wkv_st, [b_wkvst], [b_wkv[s_]])
                dma(gmh[s_], gm_d[h], [b_gmd[h]], [b_gmh[s_]], "gmh%d" % s_)
                dma(ysgh[s_], ysg_d[h], [b_ysgd[h]], [b_ysgh[s_]], "ysgh%d" % s_)
                for half in range(2):
                    tk = slice(half * 512, (half + 1) * 512)
                    bk = gen_bank()
                    for k in range(6):
                        mm(PS[bk], wq_b[s_][:, k, 0:128], qnT[:, k, tk], k == 0, k == 5, [b_wq[s_]] + b_qnT, [PSb[bk]])
                    act(qT[s_][:, tk], PS[bk], AF.Copy, [PSb[bk]], [b_qT[s_]], scale=QSCALE)
                    bk = gen_bank()
                    for k in range(6):
                        mm(PS[bk], wq_b[s_][:, k, 128:256], qnT[:, k, tk], k == 0, k == 5, [b_wq[s_]] + b_qnT, [PSb[bk]])
                    dve("tensor_tensor", [PSb[bk], b_cs1], [b_ropeA], out=ropeA, in0=PS[bk], in1=cs1T[:, tk], op=ALU.mult)
                    bk2 = gen_bank()
                    for k in range(6):
                        mm(PS[bk2], wq_b[s_][:, k, 256:384], qnT[:, k, tk], k == 0, k == 5, [b_wq[s_]] + b_qnT, [PSb[bk2]])
                    dve("tensor_tensor", [PSb[bk2], b_cs2], [PSb[bk2]], out=PS[bk2], in0=PS[bk2], in1=cs2T[:, tk],
                        op=ALU.mult)
                    dve("tensor_tensor", [PSb[bk2], b_ropeA], [b_qT[s_]], out=qrT[s_][:, tk], in0=PS[bk2], in1=ropeA,
                        op=ALU.add)

            def gen_round(h, j):
                s_ = h % 2
                ks = (h * 8 + j) % 2
                for half in range(2):
                    bk = gen_bank()
                    for c in range(4):
                        mm(PS[bk], wkv_b[s_][:, c, 0:128], kvnT_all[:, c, half * 4:(half + 1) * 4, j, :], c == 0, c == 3,
                           [b_wkv[s_]] + b_kvall[half * 4:(half + 1) * 4], [PSb[bk]])
                    copy_any(KT[ks][:, half * 4:(half + 1) * 4, :], PS[bk].rearrange("p (r t) -> p r t", r=4), [PSb[bk]],
                             [b_KT[ks]])
                for half in range(2):
                    bk = gen_bank()
                    for rr in range(4):
                        r = half * 4 + rr
                        for c in range(4):
                            mm(PS[bk][:, rr * 128:(rr + 1) * 128], kvnT_all[:, c, r, j, :], wkv_b[s_][:, c, 128:256],
                               c == 0, c == 3, [b_wkv[s_], b_kvall[r]], [PSb[bk]])
                    copy_any(Vb[ks][:, half * 4:(half + 1) * 4, 0:128], PS[bk].rearrange("p (r t) -> p r t", r=4),
                             [PSb[bk]], [b_V[ks]])

            def attn_round(h, j):
                s_ = h % 2
                ks = (h * 8 + j) % 2
                q0 = j * 128
                nq = 1024 - q0
                chunks = [(q0, nq)] if nq <= 512 else [(q0, 512), (q0 + 512, nq - 512)]
                for r in range(8):
                    for (c0, n) in chunks:
                        sb_ = 3 + st_rr[0] % 3
                        pi = st_rr[0] % 3
                        st_rr[0] += 1
                        mm(PS[sb_][:, 0:n], KT[ks][:, r, :], qT[s_][:, c0:c0 + n], True, False,
                           [b_KT[ks], b_qT[s_]], [PSb[sb_]])
                        mm(PS[sb_][:, 0:n], kropeT_all[:, r, j, :], qrT[s_][:, c0:c0 + n], False, True,
                           [b_kvall[r], b_qT[s_]], [PSb[sb_]])
                        act(PT[pi][:, 0:n], PS[sb_][:, 0:n], AF.Exp, [PSb[sb_]], [b_PT[pi]])
                        if c0 == q0:
                            dve("tensor_tensor", [b_PT[pi], b_mask], [b_PT[pi]], out=PT[pi][:, 0:128], in0=PT[pi][:, 0:128],
                                in1=maskb[:, r, :], op=ALU.mult)
                        def pv(r=r, c0=c0, n=n, pi=pi):
                            for qb in range(c0 // 128, (c0 + n) // 128):
                                first = (j == 0 and r == 0 and OC[qb] == 0)
                                last = (j == qb and r == 7)
                                mm(Oap(qb, 0, 129), PT[pi][:, qb * 128 - c0:qb * 128 - c0 + 128], Vb[ks][:, r, 0:129],
                                   first, last, [b_PT[pi], b_V[ks]], [PSb[OB[qb]]], skip=True)
                        pend.append(pv)
                        flush(2)

                def epilogue():
                    qb = j
                    e_ = (h * 8 + j) % 2
                    act(osb[e_][:, 0:129], Oap(qb, 0, 129), AF.Copy, [PSb[OB[qb]]], [b_mtmp[e_]])
                    dve("reciprocal", [b_mtmp[e_]], [b_mtmp[e_]], out=rinv[:, e_:e_ + 1], in_=osb[e_][:, 128:129])
                    dve("scalar_tensor_tensor", [b_mtmp[e_], b_gmh[s_]], [b_mtmp[e_]], out=mtmp[e_],
                        in0=osb[e_][:, 0:128], scalar=rinv[:, e_:e_ + 1], in1=gmh[s_][:, qb, :], op0=ALU.mult,
                        op1=ALU.mult)
                    dve("tensor_tensor", [b_mtmp[e_], b_ysgh[s_]], [b_mtmp[e_]], out=mrgb[e_], in0=mtmp[e_],
                        in1=ysgh[s_][:, qb, :], op=ALU.add)
                    bk = gen_bank()
                    pb = PS[bk].bitcast(BF16)
                    tr(pb[:, 0:128], mrgb[e_], ident_b, [b_mtmp[e_], b_ident_b], [PSb[bk]])
                    act(mergedT[:, h, qb * 128:(qb + 1) * 128], pb[:, 0:128], AF.Copy, [PSb[bk]], [b_mT[qb]])
                pend.append(epilogue)

            pend = []

            def flush(keep):
                while len(pend) > keep:
                    pend.pop(0)()

            NH = DEBUG.get("nheads", H)
            steps = [(h, j) for h in range(NH) for j in range(8)]
            head_prologue(0)
            gen_round(0, 0)
            for i, (h, j) in enumerate(steps):
                flush(0)
                if i + 1 < len(steps):
                    h2, j2 = steps[i + 1]
                    if j2 == 0:
                        head_prologue(h2)
                    gen_round(h2, j2)
                attn_round(h, j)
            flush(0)
            if "mergedT" in dbg_out:
                dma(dbg_out["mergedT"][:, :, :], mergedT, b_mT, [], "dbg")

            if STOP < 8:
                raise _Stop()
            xnew = R1.t[:, 0:16384].rearrange("p (b n) -> p b n", b=NB)
            gate_bc = R1.t[:, 16384:18432]
            fg_bc = R1.t[:, 18432:20480]
            b_xnew = [Buf("xnew%d" % b) for b in range(NB)]
            b_gate, b_fg = Buf("gate"), Buf("fg")
            fence(b_kvall, b_xnew + [b_gate, b_fg])
            dma(gate_bc, mod_d[0:1, 2 * D:3 * D].broadcast_to([128, D]), [b_modd], [b_gate], "c9")
            dma(fg_bc, fng_d[0:1, :].broadcast_to([128, D]), [], [b_fg], "c9")
            R3.reset()
            w_st = [R3.f32(2 * GW).rearrange("p (k n) -> p k n", k=2) for _ in range(NST)] + w_st_x
            w_bf = [R3.bf16(16 * GW).rearrange("p (k n) -> p k n", k=16) for _ in range(2)]
            fence([b_wqst, b_wkvst] + b_wq + b_wkv + b_qT + b_gmh + b_ysgh, b_wst + [q for l in b_wbf for q in l])
            xr = [R5.t[:, i * 512:(i + 1) * 512] for i in range(3)]
            otmp = [R5.t[:, 1536 + i * 512:1536 + (i + 1) * 512] for i in range(2)]
            b_xr = [Buf("xr%d" % i) for i in range(3)]
            b_otmp = [Buf("otmp%d" % i) for i in range(2)]
            fence(b_qnT, b_xr + b_otmp)
            sq2 = R6.t[:, 0:512]
            b_sq2 = Buf("sq2")
            fence(b_KT + b_V + b_PT + b_mtmp + [b_ropeA], [b_sq2] + b_wst_x)
            cnt = 0
            for n in range(4):
                wt, wb = load_weight_cg(wout_d, [(n * 512, 512)])
                for b in range(NB):
                    xi = cnt % 3
                    oi = cnt % 2
                    cnt += 1
                    dma(xr[xi], x_d[b * 128:(b + 1) * 128, n * 512:(n + 1) * 512], [], [b_xr[xi]], "xr%d" % xi)
                    bk = next_bank()
                    for k in range(16):
                        mm(PS[bk], mergedT[:, k, b * 128:(b + 1) * 128], wt[:, k, 0:512], k == 0, k == 15,
                           [b_mT[b], wb[k // 2]], [PSb[bk]])
                    dve("tensor_tensor", [PSb[bk], b_gate], [b_otmp[oi]], out=otmp[oi], in0=PS[bk],
                        in1=gate_bc[:, n * 512:(n + 1) * 512], op=ALU.mult)
                    dve("tensor_tensor", [b_otmp[oi], b_xr[xi]], [b_xnew[b]], out=xnew[:, b, n * 512:(n + 1) * 512],
                        in0=otmp[oi], in1=xr[xi], op=ALU.add)
                    act(sq2, xnew[:, b, n * 512:(n + 1) * 512], AF.Square, [b_xnew[b]], [b_sq2, b_stf[b]],
                        accum_out=st_f4[:, b, n:n + 1])
            for b in range(NB):
                dve("tensor_reduce", [b_stf[b]], [b_stf[b]], out=st_f[:, b:b + 1], in_=st_f4[:, b, :], axis=AX.X, op=ALU.add)
                rstd_from_ssq(st_f[:, b:b + 1], b_stf[b], D)
                dve("scalar_tensor_tensor", [b_xnew[b], b_stf[b], b_fg], [b_xnew[b]], out=xnew[:, b, :], in0=xnew[:, b, :],
                    scalar=st_f[:, b:b + 1], in1=fg_bc, op0=ALU.mult, op1=ALU.mult)
                dma(out_d[b * 128:(b + 1) * 128, :], xnew[:, b, :], [b_xnew[b]], [], "out%d" % (b % 2))
        except _Stop:
            pass
        P.emit(es)
    return nc, P


def _prep_inputs(inputs):
    f = lambda k: np.ascontiguousarray(np.asarray(inputs[k]), dtype=np.float32)
    x = f("x").reshape(S // 128, 128, D)
    pos = np.ascontiguousarray(np.asarray(inputs["positions"]), dtype=np.int32).reshape(S // 128, 128)
    c = f("c").reshape(16, 128).T.copy()
    w_uq = f("w_uq").reshape(QL, H, 192)
    nope = w_uq[:, :, 0:128]
    rope = w_uq[:, :, 128:192]
    ropesw = np.concatenate([w_uq[:, :, 160:192], w_uq[:, :, 128:160]], axis=2)
    wuq = np.ascontiguousarray(np.concatenate([nope, rope, rope, ropesw, ropesw], axis=2).transpose(1, 0, 2))
    wukv = np.ascontiguousarray(f("w_ukv").reshape(KVL, H, 256).transpose(1, 0, 2))
    wsT = np.ascontiguousarray(f("w_spatial").reshape(16, 128, 128).transpose(2, 0, 1))
    bspT = np.ascontiguousarray(f("b_spatial").reshape(16, 128).T)
    tri = np.triu(np.ones((128, 128), np.float32))
    ident = np.eye(128, dtype=np.float32)
    invf = (1.0 / (10000.0 ** (np.arange(0, 64, 2, dtype=np.float32) / np.float32(64)))).astype(np.float32)
    invf_bc = np.ascontiguousarray(np.broadcast_to(invf[None, :], (128, 32)))
    w_in = f("w_in").reshape(D, INW)
    cols = list(range(0, 1344)) + list(range(OFF_V, OFF_V + D))
    for g in range(H):
        for off in (OFF_U, OFF_ZS, OFF_GS, OFF_ZM, OFF_GM):
            cols += list(range(off + g * 128, off + (g + 1) * 128))
    w_in_p = np.ascontiguousarray(w_in[:, np.asarray(cols)])
    shared = {
        "cT": c, "w_ada": f("w_ada").reshape(D, 3 * D), "b_ada": f("b_ada").reshape(1, 3 * D),
        "attn_norm_g": f("attn_norm_g").reshape(1, D), "w_in": w_in_p,
        "q_norm_g": f("q_norm_g").reshape(1, QL), "kv_norm_g": f("kv_norm_g").reshape(1, KVL),
        "w_uq": wuq, "w_ukv": wukv, "sgu_norm_g": f("sgu_norm_g").reshape(1, D), "wsT": wsT, "bspT": bspT,
        "w_out": f("w_out").reshape(D, D), "final_norm_g": f("final_norm_g").reshape(1, D),
        "tri": tri, "ident": ident, "invf": invf_bc,
    }
    in_maps = []
    for i in range(NCORES):
        blocks = [8 * j + i for j in range(NB)]
        m = dict(shared)
        m["x"] = np.ascontiguousarray(x[blocks].reshape(TPC, D))
        m["pos"] = np.ascontiguousarray(pos[blocks].T)
        mask = np.zeros((128, 8, 128), np.float32)
        for r in range(8):
            if r < i:
                mask[:, r, :] = 1.0
            elif r == i:
                mask[:, r, :] = tri
        m["mask"] = mask
        in_maps.append(m)
    return in_maps


def kernel(**inputs):
    in_maps = _prep_inputs(inputs)
    dbg = DEBUG.get("dbg", ())
    nc, P = build_program(dbg)
    res = run_bass_kernel_spmd(nc, in_maps, core_ids=list(range(NCORES)))
    out = np.empty((S // 128, 128, D), np.float32)
    for i in range(NCORES):
        oc = np.asarray(res.results[i]["out"]).reshape(NB, 128, D)
        for j in range(NB):
            out[8 * j + i] = oc[j]
    if dbg:
        DEBUG["results"] = res.results
    return out.reshape(1, S, D)
```

```python
import math
from contextlib import ExitStack

import numpy as np
import concourse.bass as bass
import concourse.mybir as mybir
from concourse.bass_utils import run_bass_kernel_spmd

F32 = mybir.dt.float32
BF16 = mybir.dt.bfloat16
I32 = mybir.dt.int32
AF = mybir.ActivationFunctionType
ALU = mybir.AluOpType
AX = mybir.AxisListType

NCORES = 8
D = 2048
S = 8192
TPC = S // NCORES
NB = TPC // 128
H = 16
QL, KVL, RD = 768, 512, 64
INW = 13632
EPS = 1e-6
DEBUG = {}


class Buf:
    __slots__ = ("name", "w", "r")

    def __init__(self, name):
        self.name = name
        self.w = None
        self.r = {}


class Ins:
    __slots__ = ("eng", "fn", "deps", "slot", "sig", "need_sig", "idx")

    def __init__(self, eng, fn, slot):
        self.eng = eng
        self.fn = fn
        self.slot = slot
        self.deps = set()
        self.sig = None
        self.need_sig = False


class Prog:
    ENGS = ("pe", "act", "dve", "pool", "sp")

    def __init__(self, nc):
        self.nc = nc
        self.ins = []

    def op(self, eng, fn, reads=(), writes=(), slot=None):
        I = Ins(eng, fn, slot)
        I.idx = len(self.ins)
        key = eng if slot is None else ("dma", slot)
        psr = [b for b in reads if b.name.startswith("ps")]
        if psr:
            reads = [b for b in reads if not b.name.startswith("ps")]
            writes = list(writes) + [b for b in psr if b not in writes]
        deps = set()
        for b in reads:
            Dp = b.w
            if Dp is not None and Dp is not I:
                if not (Dp.slot is None and slot is None and Dp.eng == "pe" and eng == "pe"):
                    deps.add(Dp)
        for b in writes:
            Dp = b.w
            if Dp is not None and Dp is not I:
                if not (Dp.slot is None and slot is None and Dp.eng == eng):
                    deps.add(Dp)
            for R in b.r.values():
                if R is I:
                    continue
                if R.slot is None and slot is None and R.eng == eng:
                    continue
                deps.add(R)
        for b in reads:
            b.r[key] = I
        for b in writes:
            b.w = I
            b.r = {}
        I.deps = deps
        self.ins.append(I)
        return I

    def emit(self, es):
        nc = self.nc
        for I in self.ins:
            for Dp in I.deps:
                Dp.need_sig = True
        esem = {e: es.enter_context(nc.semaphore("sem_" + e)) for e in self.ENGS}
        slots = []
        seen = set()
        for I in self.ins:
            if I.slot is not None and I.slot not in seen:
                seen.add(I.slot)
                slots.append(I.slot)
        ssem = {s: es.enter_context(nc.semaphore("dsem_%d" % i)) for i, s in enumerate(slots)}
        ecnt = {e: 0 for e in self.ENGS}
        scnt = {s: 0 for s in slots}
        for I in self.ins:
            if I.slot is not None:
                inc = 1 if I.slot == "cc" else 16
                scnt[I.slot] += inc
                I.sig = (ssem[I.slot], scnt[I.slot], inc)
            elif I.need_sig:
                ecnt[I.eng] += 1
                I.sig = (esem[I.eng], ecnt[I.eng], 1)
        per_eng = {e: [] for e in self.ENGS}
        for I in self.ins:
            per_eng[I.eng].append(I)
        final = [(ssem[s], scnt[s]) for s in slots]
        self.stats = {e: len(per_eng[e]) for e in self.ENGS}
        self.stats["nsem"] = len(slots) + len(self.ENGS)
        self.stats["sig"] = dict(ecnt)

        def run_engine(ename, handle):
            waited = {}
            for I in per_eng[ename]:
                need = {}
                for Dp in I.deps:
                    sem, val, _ = Dp.sig
                    k = id(sem)
                    if waited.get(k, 0) >= val:
                        continue
                    if k not in need or need[k][1] < val:
                        need[k] = (sem, val)
                for k, (sem, val) in need.items():
                    handle.wait_ge(sem, val)
                    waited[k] = val
                bi = I.fn(handle)
                if I.sig is not None:
                    bi.then_inc(I.sig[0], I.sig[2])
            if ename == "sp":
                for sem, val in final:
                    if val > 0:
                        handle.wait_ge(sem, val)

        with nc.Block() as block:
            @block.tensor
            def _(h):
                run_engine("pe", h)

            @block.scalar
            def _(h):
                run_engine("act", h)

            @block.vector
            def _(h):
                run_engine("dve", h)

            @block.gpsimd
            def _(h):
                run_engine("pool", h)

            @block.sync
            def _(h):
                run_engine("sp", h)


PI = math.pi
QSCALE = 1.0 / math.sqrt(192.0)
GW = 640
OFF_Q, OFF_KV, OFF_KR, OFF_ZM, OFF_U, OFF_V, OFF_ZS, OFF_GM, OFF_GS = (
    0, 768, 1280, 1344, 3392, 5440, 7488, 9536, 11584)


class Region:
    def __init__(self, t, ncols):
        self.t = t
        self.n = ncols
        self.off = 0

    def reset(self, off=0):
        self.off = off

    def f32(self, ncols):
        assert self.off + ncols <= self.n, (self.off, ncols, self.n)
        ap = self.t[:, self.off:self.off + ncols]
        self.off += ncols
        return ap

    def bf16(self, nelem):
        nc32 = (nelem + 1) // 2
        return self.f32(nc32).bitcast(BF16)


class _Stop(Exception):
    pass


def build_program(dbg=()):
    STOP = DEBUG.get("stop", 99)
    nc = bass.Bass("TRN2", target_bir_lowering=False)
    P = Prog(nc)

    def din(name, shape, dt=F32):
        return nc.dram_tensor(name, list(shape), dt, kind="ExternalInput")

    x_d = din("x", [TPC, D])
    pos_d = din("pos", [128, NB], I32)
    c_d = din("cT", [128, 16])
    wada_d = din("w_ada", [D, 3 * D])
    bada_d = din("b_ada", [1, 3 * D])
    gattn_d = din("attn_norm_g", [1, D])
    win_d = din("w_in", [D, INW])
    qg_d = din("q_norm_g", [1, QL])
    kvg_d = din("kv_norm_g", [1, KVL])
    wuq_d = din("w_uq", [H, QL, 384])
    wukv_d = din("w_ukv", [H, KVL, 256])
    gv_d = din("sgu_norm_g", [1, D])
    wsT_d = din("wsT", [128, 16, 128])
    bspT_d = din("bspT", [128, 16])
    wout_d = din("w_out", [D, D])
    fng_d = din("final_norm_g", [1, D])
    mask_d = din("mask", [128, 8, 128])
    tri_d = din("tri", [128, 128])
    ident_d = din("ident", [128, 128])
    invf_d = din("invf", [128, 32])
    out_d = nc.dram_tensor("out", [TPC, D], F32, kind="ExternalOutput")
    mod_d = nc.dram_tensor("mod_scr", [1, 3 * D], F32)
    gm_d = nc.dram_tensor("gm_scr", [H, 128, NB, 128], F32)
    ysg_d = nc.dram_tensor("ysg_scr", [H, 128, NB, 128], F32)
    agin_d = nc.dram_tensor("ag_in", [576, TPC], BF16)
    agout_d = nc.dram_tensor("ag_out", [NCORES * 576, TPC], BF16)
    dbg_out = {}
    for name, shape, dt in dbg:
        dbg_out[name] = nc.dram_tensor("dbg_" + name, list(shape), dt, kind="ExternalOutput")

    es = ExitStack()
    with es:
        try:
            def sbt(name, ncols):
                return es.enter_context(nc.sbuf_tensor(name, [128, ncols], F32))

            R1 = Region(sbt("R1", 20480), 20480)
            R2 = Region(sbt("R2", 8192), 8192)
            R3 = Region(sbt("R3", 12800), 12800)
            R5 = Region(sbt("R5", 3072), 3072)
            R6 = Region(sbt("R6", 4096), 4096)
            R7 = Region(sbt("R7", 3584), 3584)
            psum = es.enter_context(nc.psum_tensor("ps", [128, 4096], F32))
            PS = [psum[:, k * 512:(k + 1) * 512] for k in range(8)]
            PSb = [Buf("ps%d" % k) for k in range(8)]
            ps_rr = [0]

            def next_bank(banks=range(8)):
                banks = list(banks)
                k = banks[ps_rr[0] % len(banks)]
                ps_rr[0] += 1
                return k

            def fence(olds, news, eng="pool"):
                olds = list(olds)
                P.op(eng, lambda e: e.memset(fscr[:, 0:1], 0.0), reads=olds, writes=olds + list(news) + [fscr_b])

            def dma(out, in_, reads, writes, slot, eng="sp"):
                return P.op(eng, lambda e: e.dma_start(out=out, in_=in_), reads=reads, writes=writes, slot=slot)

            def mm(out, lhsT, rhs, start, stop, reads, writes, skip=False):
                if skip:
                    return P.op("pe", lambda e: e.matmul(out, lhsT, rhs, start=start, stop=stop,
                                                         skip_group_check=True), reads=reads, writes=writes)
                return P.op("pe", lambda e: e.matmul(out, lhsT, rhs, start=start, stop=stop),
                            reads=reads, writes=writes)

            def tr(out, in_, ident, reads, writes):
                return P.op("pe", lambda e: e.transpose(out, in_, ident), reads=reads, writes=writes)

            def act(out, in_, func, reads, writes, **kw):
                return P.op("act", lambda e: e.activation(out=out, in_=in_, func=func, **kw), reads=reads, writes=writes)

            def dve(name, reads, writes, *a, **kw):
                return P.op("dve", lambda e: getattr(e, name)(*a, **kw), reads=reads, writes=writes)

            cp_rr = [0]

            def copy_any(out, in_, reads, writes, scale=None):
                cp_rr[0] += 1
                if scale is not None or cp_rr[0] % 2 == 0:
                    kw = {} if scale is None else {"scale": scale}
                    return act(out, in_, AF.Copy, reads, writes, **kw)
                return dve("tensor_copy", reads, writes, out=out, in_=in_)

            def rstd_from_ssq(st_ap, buf, n):
                act(st_ap, st_ap, AF.Sqrt, [buf], [buf], scale=1.0 / n, bias=EPS)
                dve("reciprocal", [buf], [buf], out=st_ap, in_=st_ap)

            ident_f = R7.f32(128)
            ident_b = R7.bf16(128)
            tri_b = R7.bf16(128)
            maskb = R7.bf16(1024).rearrange("p (r q) -> p r q", r=8)
            bspT = R7.f32(16)
            invf = R7.f32(32)
            pos_i = R7.f32(8).bitcast(I32)
            pos_f = R7.f32(8)
            cos_t = R7.f32(256).rearrange("p (b d) -> p b d", b=8)
            sin_t = R7.f32(256).rearrange("p (b d) -> p b d", b=8)
            cs1T = R7.f32(1024)
            cs2T = R7.f32(1024)
            cT = R7.f32(16)
            csig = R7.f32(16)
            cact = R7.f32(16)
            st_x = R7.f32(8)
            st_q = R7.f32(8)
            st_kv = R7.f32(8)
            st_v4 = R7.f32(32).rearrange("p (b n) -> p b n", b=8)
            st_v = R7.f32(8)
            st_f4 = R7.f32(32).rearrange("p (b n) -> p b n", b=8)
            st_f = R7.f32(8)
            rinv = R7.f32(16)
            fscr = R7.f32(4)
            fscr_b = Buf("fscr")
            b_ident_f, b_ident_b, b_tri, b_mask, b_bsp, b_invf = (Buf(n) for n in
                                                                  ("identf", "identb", "tri", "mask", "bsp", "invf"))
            b_pos, b_cos, b_sin, b_cs1, b_cs2, b_c = (Buf(n) for n in ("pos", "cos", "sin", "cs1", "cs2", "c"))
            b_stx = [Buf("stx%d" % b) for b in range(NB)]
            b_stq = [Buf("stq%d" % b) for b in range(NB)]
            b_stkv = [Buf("stkv%d" % b) for b in range(NB)]
            b_stv = [Buf("stv%d" % b) for b in range(NB)]
            b_stf = [Buf("stf%d" % b) for b in range(NB)]

            R1.reset()
            mask_st = R1.f32(1024).rearrange("p (r q) -> p r q", r=8)
            tri_st = R1.f32(128)
            b_tmp0 = Buf("tmp0")
            dma(ident_f, ident_d[:, :], [], [b_ident_f], "c0")
            dma(mask_st, mask_d[:, :, :], [], [b_tmp0], "c1")
            dma(tri_st, tri_d[:, :], [], [b_tmp0], "c1")
            dma(bspT, bspT_d[:, :], [], [b_bsp], "c2")
            dma(invf, invf_d[:, :], [], [b_invf], "c2")
            dma(pos_i, pos_d[:, :], [], [b_pos], "c2")
            dma(cT, c_d[:, :], [], [b_c], "c2")
            dve("tensor_copy", [b_ident_f], [b_ident_b], out=ident_b, in_=ident_f)
            dve("tensor_copy", [b_tmp0], [b_mask], out=maskb, in_=mask_st)
            dve("tensor_copy", [b_tmp0], [b_tri], out=tri_b, in_=tri_st)
            dve("tensor_copy", [b_pos], [b_pos], out=pos_f, in_=pos_i)

            act(csig, cT, AF.Sigmoid, [b_c], [b_c])
            dve("tensor_tensor", [b_c], [b_c], out=cact, in0=cT, in1=csig, op=ALU.mult)
            R1.reset(1152)
            wa_st = [R1.f32(8192).rearrange("p (k n) -> p k n", k=16) for _ in range(2)]
            b_wa = [Buf("wa%d" % i) for i in range(2)]
            R2.reset()
            modrow = R2.f32(8192)[0:1, 0:6144]
            b_modrow = Buf("modrow")
            R3.reset()
            badarow = R3.f32(6144)[0:1, :]
            b_bada = Buf("bada")
            dma(badarow, bada_d[0:1, :], [], [b_bada], "c3")
            wada_v = wada_d.ap().rearrange("(k p) n -> p k n", p=128)
            for n in range(12):
                s_ = n % 2
                for hh in range(2):
                    dma(wa_st[s_][:, hh * 8:(hh + 1) * 8, :], wada_v[:, hh * 8:(hh + 1) * 8, n * 512:(n + 1) * 512],
                        [], [b_wa[s_]], "wa%d" % s_)
                bk = next_bank()
                for k in range(16):
                    mm(PS[bk][0:1, :], cact[:, k:k + 1], wa_st[s_][:, k, :], k == 0, k == 15,
                       [b_c, b_wa[s_]], [PSb[bk]])
                dve("tensor_tensor", [PSb[bk], b_bada], [b_modrow], out=modrow[:, n * 512:(n + 1) * 512],
                    in0=PS[bk][0:1, :], in1=badarow[:, n * 512:(n + 1) * 512], op=ALU.add)
            b_modd = Buf("mod_d")
            dma(mod_d[0:1, :], modrow, [b_modrow], [b_modd], "c4")

            R3.reset(6144)
            ang = R3.f32(256).rearrange("p (b d) -> p b d", b=8)
            sarg = R3.f32(256)
            carg = R3.f32(256)
            T1 = R3.f32(1024).rearrange("p (b d) -> p b d", b=8)
            T2 = R3.f32(1024).rearrange("p (b d) -> p b d", b=8)
            b_ang, b_T = Buf("ang"), Buf("T12")
            for b in range(NB):
                dve("tensor_scalar", [b_invf, b_pos], [b_ang], out=ang[:, b, :], in0=invf, scalar1=pos_f[:, b:b + 1],
                    scalar2=None, op0=ALU.mult)
            angf = ang.rearrange("p b d -> p (b d)")
            MAGIC = 12582912.0
            C1 = 6.28125
            C2 = 2.0 * PI - 6.28125
            nfs = R3.f32(256)
            for (dst, dbuf, r_ap, phase) in ((sin_t, b_sin, sarg, 0.0), (cos_t, b_cos, carg, 0.25)):
                dve("tensor_scalar", [b_ang], [b_ang], out=nfs, in0=angf, scalar1=1.0 / (2.0 * PI), scalar2=phase,
                    op0=ALU.mult, op1=ALU.add)
                dve("tensor_scalar", [b_ang], [b_ang], out=nfs, in0=nfs, scalar1=MAGIC, scalar2=None, op0=ALU.add)
                dve("tensor_scalar", [b_ang], [b_ang], out=nfs, in0=nfs, scalar1=-MAGIC, scalar2=None, op0=ALU.add)
                dve("scalar_tensor_tensor", [b_ang], [b_ang], out=r_ap, in0=nfs, scalar=-C1, in1=angf,
                    op0=ALU.mult, op1=ALU.add)
                dve("scalar_tensor_tensor", [b_ang], [b_ang], out=r_ap, in0=nfs, scalar=-C2, in1=r_ap,
                    op0=ALU.mult, op1=ALU.add)
                dve("tensor_scalar", [b_ang], [b_ang], out=r_ap, in0=r_ap, scalar1=phase * 2.0 * PI, scalar2=-PI,
                    op0=ALU.add, op1=ALU.max)
                dve("tensor_scalar", [b_ang], [b_ang], out=r_ap, in0=r_ap, scalar1=PI, scalar2=None, op0=ALU.min)
                act(dst.rearrange("p b d -> p (b d)"), r_ap, AF.Sin, [b_ang], [dbuf])
            for rr in range(4):
                dve("tensor_copy", [b_cos], [b_T], out=T1[:, :, rr * 32:(rr + 1) * 32], in_=cos_t)
                if rr % 2 == 0:
                    dve("tensor_scalar", [b_sin], [b_T], out=T2[:, :, rr * 32:(rr + 1) * 32], in0=sin_t, scalar1=-1.0,
                        scalar2=None, op0=ALU.mult)
                else:
                    dve("tensor_copy", [b_sin], [b_T], out=T2[:, :, rr * 32:(rr + 1) * 32], in_=sin_t)
            for (Tt, dst, bd) in ((T1, cs1T, b_cs1), (T2, cs2T, b_cs2)):
                for half in range(2):
                    bk = next_bank()
                    for bb in range(4):
                        b = half * 4 + bb
                        tr(PS[bk][:, bb * 128:(bb + 1) * 128], Tt[:, b, :], ident_f, [b_T, b_ident_f], [PSb[bk]])
                    act(dst[:, half * 512:(half + 1) * 512], PS[bk][:, :], AF.Copy, [PSb[bk]], [bd], scale=QSCALE)

            R1.reset()
            x_st = [R1.f32(2048) for _ in range(2)]
            b_xst = [Buf("xst%d" % i) for i in range(2)]
            hf = R1.f32(2048)
            hb = R1.bf16(2048)
            junk = R1.f32(2048)
            A_bc = R1.f32(2048)
            shift_bc = R1.f32(2048)
            gattn_bc = R1.f32(2048)
            b_hf, b_hb, b_junk, b_A, b_shift, b_gattn = (Buf(n) for n in ("hf", "hb", "junk", "A", "shift", "gattn"))
            fence([b_wa[0], b_wa[1], b_tmp0], [b_xst[0], b_xst[1], b_hf, b_hb, b_junk, b_A, b_shift, b_gattn])
            dma(A_bc, mod_d[0:1, D:2 * D].broadcast_to([128, D]), [b_modd], [b_A], "c5")
            dma(shift_bc, mod_d[0:1, 0:D].broadcast_to([128, D]), [b_modd], [b_shift], "c5")
            dma(gattn_bc, gattn_d[0:1, :].broadcast_to([128, D]), [], [b_gattn], "c5")
            dve("scalar_tensor_tensor", [b_A, b_gattn], [b_A], out=A_bc, in0=A_bc, scalar=1.0, in1=gattn_bc,
                op0=ALU.add, op1=ALU.mult)
            hT = R2.t[:, 0:8192].bitcast(BF16).rearrange("p (k t) -> p k t", k=16)
            b_hT = [Buf("hT%d" % b) for b in range(NB)]
            fence([b_modrow], b_hT)
            for b in range(NB):
                s_ = b % 2
                dma(x_st[s_], x_d[b * 128:(b + 1) * 128, :], [], [b_xst[s_]], "xst%d" % s_)
                act(junk, x_st[s_], AF.Square, [b_xst[s_]], [b_junk, b_stx[b]], accum_out=st_x[:, b:b + 1])
                rstd_from_ssq(st_x[:, b:b + 1], b_stx[b], D)
                dve("scalar_tensor_tensor", [b_xst[s_], b_stx[b], b_A], [b_hf], out=hf, in0=x_st[s_],
                    scalar=st_x[:, b:b + 1], in1=A_bc, op0=ALU.mult, op1=ALU.mult)
                dve("tensor_tensor", [b_hf, b_shift], [b_hb], out=hb, in0=hf, in1=shift_bc, op=ALU.add)
                for half in range(2):
                    bk = next_bank()
                    pb = PS[bk].bitcast(BF16)
                    for kk in range(8):
                        k = half * 8 + kk
                        tr(pb[:, kk * 128:(kk + 1) * 128], hb[:, k * 128:(k + 1) * 128], ident_b,
                           [b_hb, b_ident_b], [PSb[bk]])
                    copy_any(hT[:, half * 8:(half + 1) * 8, b * 128:(b + 1) * 128],
                             pb.rearrange("p (k t) -> p k t", k=8), [PSb[bk]], [b_hT[b]])

            if STOP < 4:
                raise _Stop()
            R3.reset()
            NST = 2
            w_st = [R3.f32(2 * GW).rearrange("p (k n) -> p k n", k=2) for _ in range(NST)]
            b_wst = [Buf("wst%d" % i) for i in range(NST)]
            w_bf = [R3.bf16(16 * GW).rearrange("p (k n) -> p k n", k=16) for _ in range(2)]
            b_wbf = [[Buf("wbf%d_%d" % (i, q)) for q in range(8)] for i in range(2)]
            fence([b_bada, b_ang, b_T], b_wst + [q for l in b_wbf for q in l])
            wst_rr = [0]
            nst_active = [NST]
            w_st_x = [R6.t[:, 1024 + i * 2 * GW:1024 + (i + 1) * 2 * GW].rearrange("p (k n) -> p k n", k=2)
                      for i in range(2)]
            b_wst_x = [Buf("wst%d" % (NST + i)) for i in range(2)]
            wbf_rr = [0]

            def load_weight_cg(src, pieces, lazy=False):
                slot = wbf_rr[0] % 2
                wbf_rr[0] += 1
                wtot = sum(w for _, w in pieces)
                v = src.ap().rearrange("(k p) n -> p k n", p=128)
                def piece(q):
                    ss = wst_rr[0] % nst_active[0]
                    wst_rr[0] += 1
                    c = 0
                    for (c0, w) in pieces:
                        dma(w_st[ss][:, :, c:c + w], v[:, 2 * q:2 * q + 2, c0:c0 + w], [], [b_wst[ss]], "wst%d" % ss)
                        c += w
                    copy_any(w_bf[slot][:, 2 * q:2 * q + 2, 0:wtot], w_st[ss][:, :, 0:wtot], [b_wst[ss]],
                             [b_wbf[slot][q]])
                if lazy:
                    return w_bf[slot], b_wbf[slot], piece
                for q in range(8):
                    piece(q)
                return w_bf[slot], b_wbf[slot]

            R1.reset()
            lat = R1.f32(NB * 1344).rearrange("p (b n) -> p b n", b=NB)
            b_lat = [Buf("lat%d" % b) for b in range(NB)]
            v_bf = R1.bf16(NB * 2048).rearrange("p (b n) -> p b n", b=NB)
            b_vbf = [Buf("vbf%d" % b) for b in range(NB)]
            qg_bc = R1.f32(768)
            kvg_bc = R1.f32(512)
            b_qg, b_kvg = Buf("qg"), Buf("kvg")
            fence([b_xst[0], b_xst[1], b_hf, b_hb, b_junk, b_A, b_shift, b_gattn], b_lat + b_vbf + [b_qg, b_kvg])
            dma(qg_bc, qg_d[0:1, :].broadcast_to([128, QL]), [], [b_qg], "c6")
            dma(kvg_bc, kvg_d[0:1, :].broadcast_to([128, KVL]), [], [b_kvg], "c6")
            qnT = R5.t[:, 0:3072].bitcast(BF16).rearrange("p (k t) -> p k t", k=6)
            b_qnT = [Buf("qnT%d" % b) for b in range(NB)]
            R6.reset()
            kvnT_own = R6.bf16(4 * 1024).rearrange("p (k t) -> p k t", k=4)
            kropeT_own = R6.bf16(1024)
            b_kvown = [Buf("kvown%d" % b) for b in range(NB)]
            qn_b = R6.bf16(768)
            kvn_b = R6.bf16(512)
            krot = R6.bf16(64)
            rt = [R6.f32(32) for _ in range(4)]
            sqj = R6.bf16(768)
            b_qnb, b_kvnb, b_krot, b_rt, b_sqj = Buf("qnb"), Buf("kvnb"), Buf("krot"), Buf("rt"), Buf("sqj")

            def inproj_cg(wt, wb, width, evac):
                for b in range(NB):
                    bk = next_bank()
                    for k in range(16):
                        mm(PS[bk][:, 0:width], hT[:, k, b * 128:(b + 1) * 128], wt[:, k, 0:width], k == 0, k == 15,
                           [b_hT[b], wb[k // 2]], [PSb[bk]])
                    evac(b, bk)

            for (c0, w) in ((0, 512), (512, 512), (1024, 320)):
                wt, wb = load_weight_cg(win_d, [(c0, w)])

                def evac_lat(b, bk, c0=c0, w=w):
                    copy_any(lat[:, b, c0:c0 + w], PS[bk][:, 0:w], [PSb[bk]], [b_lat[b]])
                inproj_cg(wt, wb, w, evac_lat)

            def lat_post(b):
                act(sqj, lat[:, b, 0:768], AF.Square, [b_lat[b]], [b_sqj, b_stq[b]], accum_out=st_q[:, b:b + 1])
                rstd_from_ssq(st_q[:, b:b + 1], b_stq[b], QL)
                dve("scalar_tensor_tensor", [b_lat[b], b_stq[b], b_qg], [b_qnb], out=qn_b, in0=lat[:, b, 0:768],
                    scalar=st_q[:, b:b + 1], in1=qg_bc, op0=ALU.mult, op1=ALU.mult)
                bk = next_bank()
                pb = PS[bk].bitcast(BF16)
                for k in range(6):
                    tr(pb[:, k * 128:(k + 1) * 128], qn_b[:, k * 128:(k + 1) * 128], ident_b, [b_qnb, b_ident_b],
                       [PSb[bk]])
                copy_any(qnT[:, :, b * 128:(b + 1) * 128], pb[:, 0:768].rearrange("p (k t) -> p k t", k=6),
                         [PSb[bk]], [b_qnT[b]])
                act(sqj[:, 0:512], lat[:, b, 768:1280], AF.Square, [b_lat[b]], [b_sqj, b_stkv[b]],
                    accum_out=st_kv[:, b:b + 1])
                rstd_from_ssq(st_kv[:, b:b + 1], b_stkv[b], KVL)
                dve("scalar_tensor_tensor", [b_lat[b], b_stkv[b], b_kvg], [b_kvnb], out=kvn_b, in0=lat[:, b, 768:1280],
                    scalar=st_kv[:, b:b + 1], in1=kvg_bc, op0=ALU.mult, op1=ALU.mult)
                t1 = lat[:, b, 1280:1312]
                t2 = lat[:, b, 1312:1344]
                dve("tensor_tensor", [b_lat[b], b_cos], [b_rt], out=rt[0], in0=t1, in1=cos_t[:, b, :], op=ALU.mult)
                dve("tensor_tensor", [b_lat[b], b_sin], [b_rt], out=rt[1], in0=t2, in1=sin_t[:, b, :], op=ALU.mult)
                dve("tensor_tensor", [b_lat[b], b_cos], [b_rt], out=rt[2], in0=t2, in1=cos_t[:, b, :], op=ALU.mult)
                dve("tensor_tensor", [b_lat[b], b_sin], [b_rt], out=rt[3], in0=t1, in1=sin_t[:, b, :], op=ALU.mult)
                dve("tensor_tensor", [b_rt], [b_krot], out=krot[:, 0:32], in0=rt[0], in1=rt[1], op=ALU.subtract)
                dve("tensor_tensor", [b_rt], [b_krot], out=krot[:, 32:64], in0=rt[2], in1=rt[3], op=ALU.add)
                bk = next_bank()
                pb = PS[bk].bitcast(BF16)
                for k in range(4):
                    tr(pb[:, k * 128:(k + 1) * 128], kvn_b[:, k * 128:(k + 1) * 128], ident_b, [b_kvnb, b_ident_b],
                       [PSb[bk]])
                tr(pb[0:64, 512:640], krot[:, 0:64], ident_b, [b_krot, b_ident_b], [PSb[bk]])
                copy_any(kvnT_own[:, :, b * 128:(b + 1) * 128], pb[:, 0:512].rearrange("p (k t) -> p k t", k=4),
                         [PSb[bk]], [b_kvown[b]])
                copy_any(kropeT_own[0:64, b * 128:(b + 1) * 128], pb[0:64, 512:640], [PSb[bk]], [b_kvown[b]])

            def evac_v_factory(n):
                def evac_v(b, bk):
                    act(sqj[:, 0:512], PS[bk][:, 0:512], AF.Square, [PSb[bk]], [b_sqj, b_stv[b]],
                        accum_out=st_v4[:, b, n:n + 1])
                    dve("tensor_copy", [PSb[bk]], [b_vbf[b]], out=v_bf[:, b, n * 512:(n + 1) * 512], in_=PS[bk][:, 0:512])
                return evac_v

            for n in range(4):
                wt, wb = load_weight_cg(win_d, [(1344 + n * 512, 512)])
                inproj_cg(wt, wb, 512, evac_v_factory(n))
                if n < 2:
                    for b in range(n * 4, n * 4 + 4):
                        lat_post(b)
                if n == 1:
                    b_agin, b_agout = Buf("agin"), Buf("agout")
                    dma(agin_d[0:512, :].rearrange("(k p) t -> p k t", p=128), kvnT_own, b_kvown, [b_agin], "c7")
                    dma(agin_d[512:576, :], kropeT_own[0:64, :], b_kvown, [b_agin], "c7")
                    P.op("pool", lambda e: e.collective_compute("AllGather", ALU.bypass,
                                                                replica_groups=[list(range(NCORES))],
                                                                ins=[agin_d.ap().opt()], outs=[agout_d.ap().opt()]),
                         reads=[b_agin], writes=[b_agout], slot="cc")
            if "qnT" in dbg_out:
                dma(dbg_out["qnT"][:, :, :], qnT, b_qnT, [], "dbg")
            if "agout" in dbg_out:
                dma(dbg_out["agout"][:, :], agout_d[:, :], [b_agout], [], "dbg")

            if STOP < 5:
                raise _Stop()
            R1.reset()
            wsT_st = R1.f32(2048).rearrange("p (g t) -> p g t", g=16)
            gv_bc = R1.f32(2048)
            NT = 2
            sgs = [R1.f32(256) for _ in range(NT)]
            sgm = [R1.f32(256) for _ in range(NT)]
            ub = [R1.f32(128) for _ in range(NT)]
            tt1 = [R1.f32(128) for _ in range(NT)]
            tt2 = [R1.f32(128) for _ in range(NT)]
            tm = [R1.f32(128) for _ in range(NT)]
            ta = [R1.f32(128) for _ in range(NT)]
            gm_st = [R1.f32(1024).rearrange("p (b n) -> p b n", b=NB) for _ in range(2)]
            ysg_st = [R1.f32(1024).rearrange("p (b n) -> p b n", b=NB) for _ in range(2)]
            b_wsst, b_gv = Buf("wsst"), Buf("gv")
            b_t = [Buf("dtmp%d" % i) for i in range(NT)]
            b_gmst = [Buf("gmst%d" % i) for i in range(2)]
            b_ysgst = [Buf("ysgst%d" % i) for i in range(2)]
            fence(b_lat, [b_wsst, b_gv] + b_t + b_gmst + b_ysgst)
            wsT_b = R6.t[:, 0:1024].bitcast(BF16).rearrange("p (g t) -> p g t", g=16)
            b_ws = Buf("wsTb")
            fence(b_kvown + [b_qnb, b_kvnb, b_krot, b_rt], [b_ws])
            fence(b_kvown + [b_qnb, b_kvnb, b_krot, b_rt, b_sqj], b_wst_x)
            w_st = w_st + w_st_x
            b_wst = b_wst + b_wst_x
            nst_active[0] = NST + 2
            dma(wsT_st, wsT_d[:, :, :], [], [b_wsst], "c8")
            dma(gv_bc, gv_d[0:1, :].broadcast_to([128, D]), [], [b_gv], "c8")
            for g in range(16):
                dve("tensor_tensor", [b_wsst, b_tri], [b_ws], out=wsT_b[:, g, :], in0=wsT_st[:, g, :], in1=tri_b,
                    op=ALU.mult)
            for b in range(NB):
                dve("tensor_reduce", [b_stv[b]], [b_stv[b]], out=st_v[:, b:b + 1], in_=st_v4[:, b, :], axis=AX.X,
                    op=ALU.add)
                rstd_from_ssq(st_v[:, b:b + 1], b_stv[b], D)
                dve("scalar_tensor_tensor", [b_vbf[b], b_stv[b], b_gv], [b_vbf[b]], out=v_bf[:, b, :], in0=v_bf[:, b, :],
                    scalar=st_v[:, b:b + 1], in1=gv_bc, op0=ALU.mult, op1=ALU.mult)
            b_gmd = [Buf("gmd%d" % g) for g in range(H)]
            b_ysgd = [Buf("ysgd%d" % g) for g in range(H)]
            tcnt = [0]
            nxt = load_weight_cg(win_d, [(3392, GW)])
            for g in range(H):
                o = g * 128
                wt, wb = nxt[0], nxt[1]
                nxt = load_weight_cg(win_d, [(3392 + (g + 1) * GW, GW)], lazy=True) if g + 1 < H else None
                so = g % 2
                for b in range(NB):
                    bx = next_bank()
                    by = next_bank()
                    for k in range(16):
                        mm(PS[bx][:, 0:384], hT[:, k, b * 128:(b + 1) * 128], wt[:, k, 0:384], k == 0, k == 15,
                           [b_hT[b], wb[k // 2]], [PSb[bx]])
                        mm(PS[by][:, 0:256], hT[:, k, b * 128:(b + 1) * 128], wt[:, k, 384:640], k == 0, k == 15,
                           [b_hT[b], wb[k // 2]], [PSb[by]])
                    mm(PS[by][:, 256:384], wsT_b[:, g, :], v_bf[:, b, o:o + 128], True, True, [b_ws, b_vbf[b]], [PSb[by]])
                    ti = tcnt[0] % NT
                    tcnt[0] += 1
                    bt = b_t[ti]
                    act(sgs[ti], PS[bx][:, 128:384], AF.Sigmoid, [PSb[bx]], [bt])
                    act(sgm[ti], PS[by][:, 0:256], AF.Sigmoid, [PSb[by]], [bt])
                    act(ub[ti], PS[bx][:, 0:128], AF.Copy, [PSb[bx]], [bt])
                    dve("tensor_tensor", [PSb[bx], bt], [bt], out=tt1[ti], in0=PS[bx][:, 128:256], in1=sgs[ti][:, 0:128],
                        op=ALU.mult)
                    dve("tensor_tensor", [bt], [bt], out=tt2[ti], in0=tt1[ti], in1=sgs[ti][:, 128:256], op=ALU.mult)
                    dve("scalar_tensor_tensor", [PSb[by], bt, b_bsp], [bt], out=tm[ti], in0=PS[by][:, 256:384],
                        scalar=bspT[:, g:g + 1], in1=ub[ti], op0=ALU.add, op1=ALU.mult)
                    dve("tensor_tensor", [bt], [b_ysgst[so]], out=ysg_st[so][:, b, :], in0=tm[ti], in1=tt2[ti],
                        op=ALU.mult)
                    dve("tensor_tensor", [PSb[by], bt], [bt], out=ta[ti], in0=PS[by][:, 0:128], in1=sgm[ti][:, 0:128],
                        op=ALU.mult)
                    dve("tensor_tensor", [bt], [b_gmst[so]], out=gm_st[so][:, b, :], in0=ta[ti], in1=sgm[ti][:, 128:256],
                        op=ALU.mult)
                    if nxt is not None:
                        nxt[2](b)
                dma(gm_d[g], gm_st[so], [b_gmst[so]], [b_gmd[g]], "gmst%d" % so)
                dma(ysg_d[g], ysg_st[so], [b_ysgst[so]], [b_ysgd[g]], "ysgst%d" % so)
            if "gm" in dbg_out:
                dma(dbg_out["gm"][:, :, :, :], gm_d[:, :, :, :], b_gmd, [], "dbg")
                dma(dbg_out["ysg"][:, :, :, :], ysg_d[:, :, :, :], b_ysgd, [], "dbg")

            if STOP < 6:
                raise _Stop()
            kvnT_all = R1.t[:, 0:16384].bitcast(BF16).rearrange("p (c r j t) -> p c r j t", c=4, r=8, j=8)
            kropeT_all = R1.t[:, 16384:20480].bitcast(BF16).rearrange("p (r j t) -> p r j t", r=8, j=8)
            b_kvall = [Buf("kvall%d" % r) for r in range(NCORES)]
            fence(b_vbf + [b_wsst, b_gv, b_qg, b_kvg] + b_t + b_gmst + b_ysgst, b_kvall)
            P.op("dve", lambda e: e.memset(kropeT_all[64:128, :, :, :], 0.0), reads=[], writes=b_kvall)
            for r in range(NCORES):
                dma(kvnT_all[:, :, r, :, :].rearrange("p c j t -> p c (j t)"),
                    agout_d[r * 576:r * 576 + 512, :].rearrange("(c p) t -> p c t", p=128), [b_agout], [b_kvall[r]],
                    "kvall")
                dma(kropeT_all[0:64, r, :, :].rearrange("p j t -> p (j t)"), agout_d[r * 576 + 512:(r + 1) * 576, :],
                    [b_agout], [b_kvall[r]], "kvall")

            mergedT = R2.t[:, 0:8192].bitcast(BF16).rearrange("p (k t) -> p k t", k=16)
            b_mT = [Buf("mT%d" % b) for b in range(NB)]
            fence(b_hT, b_mT)
            R3.reset()
            wq_st = R3.f32(6 * 384).rearrange("p (k n) -> p k n", k=6)
            wkv_st = R3.f32(4 * 256).rearrange("p (k n) -> p k n", k=4)
            wq_b = [R3.bf16(6 * 384).rearrange("p (k n) -> p k n", k=6) for _ in range(2)]
            wkv_b = [R3.bf16(4 * 256).rearrange("p (k n) -> p k n", k=4) for _ in range(2)]
            qT = [R3.bf16(1024) for _ in range(2)]
            qrT = [R3.bf16(1024) for _ in range(2)]
            gmh = [R3.f32(1024).rearrange("p (b n) -> p b n", b=NB)] * 2
            ysgh = [R3.f32(1024).rearrange("p (b n) -> p b n", b=NB)] * 2
            KT2 = R3.bf16(1024).rearrange("p (r t) -> p r t", r=8)
            Vb2 = R3.bf16(8 * 130).rearrange("p (r n) -> p r n", r=8)
            R6.reset()
            KT = [R6.bf16(1024).rearrange("p (r t) -> p r t", r=8) for _ in range(2)]
            Vb = [R6.bf16(8 * 130).rearrange("p (r n) -> p r n", r=8) for _ in range(2)]
            KT.append(KT2)
            Vb.append(Vb2)
            PT = [R6.bf16(512) for _ in range(3)]
            mtmp = [R6.f32(128) for _ in range(2)]
            mrgb = [R6.bf16(128) for _ in range(2)]
            ropeA = R6.f32(512)
            osb = [R6.f32(130) for _ in range(2)]
            b_wqst, b_wkvst = Buf("wqst"), Buf("wkvst")
            b_wq = [Buf("wq%d" % i) for i in range(2)]
            b_wkv = [Buf("wkv%d" % i) for i in range(2)]
            b_qT = [Buf("qT%d" % i) for i in range(2)]
            b_gmh = [Buf("gmh0")] * 2
            b_ysgh = [Buf("ysgh0")] * 2
            b_KT = [Buf("KT%d" % i) for i in range(3)]
            b_V = [Buf("V%d" % i) for i in range(3)]
            b_PT = [Buf("PT%d" % i) for i in range(3)]
            b_mtmp = [Buf("mtmp%d" % i) for i in range(2)]
            b_ropeA = Buf("ropeA")
            b_O = [Buf("O%d" % i) for i in range(NB)]
            fence(b_wst + [q for l in b_wbf for q in l],
                  [b_wqst, b_wkvst] + b_wq + b_wkv + b_qT + b_gmh + b_ysgh)
            fence([b_ws] + b_wst_x, b_KT + b_V + b_PT + b_mtmp + [b_ropeA])
            fence(b_wst + [q for l in b_wbf for q in l], [b_KT[2], b_V[2]])
            for s_ in range(3):
                P.op("dve", lambda e, s_=s_: e.memset(Vb[s_][:, :, 128:130], 1.0), reads=[], writes=[b_V[s_]])
            OB = [0, 0, 0, 1, 1, 1, 2, 2]
            OC = [0, 129, 258, 0, 129, 258, 0, 129]

            def Oap(qb, lo, hi):
                return PS[OB[qb]][:, OC[qb] + lo:OC[qb] + hi]
            st_rr = [0]
            gen_rr = [0]

            def gen_bank():
                gen_rr[0] += 1
                return 6 + gen_rr[0] % 2

            def load_gates(h):
                dma(gmh[0], gm_d[h], [b_gmd[h]], [b_gmh[0]], "gmh0")
                dma(ysgh[0], ysg_d[h], [b_ysgd[h]], [b_ysgh[0]], "ysgh0")

            def head_prologue(h):
                s_ = h % 2
                dma(wq_st, wuq_d[h].rearrange("(k p) n -> p k n", p=128), [], [b_wqst], "wqst")
                dma(wkv_st, wukv_d[h].rearrange("(k p) n -> p k n", p=128), [], [b_wkvst], "wkvst")
                copy_any(wq_b[s_], wq_st, [b_wqst], [b_wq[s_]])
                copy_any(wkv_b[s_], wkv_st, [b_wkvst], [b_wkv[s_]])
                if h == 0:
                    load_gates(0)
                for half in range(2):
                    tk = slice(half * 512, (half + 1) * 512)
                    bk = gen_bank()
                    for k in range(6):
                        mm(PS[bk], wq_b[s_][:, k, 0:128], qnT[:, k, tk], k == 0, k == 5, [b_wq[s_]] + b_qnT, [PSb[bk]])
                    act(qT[s_][:, tk], PS[bk], AF.Copy, [PSb[bk]], [b_qT[s_]], scale=QSCALE)
                    bk = gen_bank()
                    for k in range(6):
                        mm(PS[bk], wq_b[s_][:, k, 128:256], qnT[:, k, tk], k == 0, k == 5, [b_wq[s_]] + b_qnT, [PSb[bk]])
                    dve("tensor_tensor", [PSb[bk], b_cs1], [b_ropeA], out=ropeA, in0=PS[bk], in1=cs1T[:, tk], op=ALU.mult)
                    bk2 = gen_bank()
                    for k in range(6):
                        mm(PS[bk2], wq_b[s_][:, k, 256:384], qnT[:, k, tk], k == 0, k == 5, [b_wq[s_]] + b_qnT, [PSb[bk2]])
                    dve("tensor_tensor", [PSb[bk2], b_cs2], [PSb[bk2]], out=PS[bk2], in0=PS[bk2], in1=cs2T[:, tk],
                        op=ALU.mult)
                    dve("tensor_tensor", [PSb[bk2], b_ropeA], [b_qT[s_]], out=qrT[s_][:, tk], in0=PS[bk2], in1=ropeA,
                        op=ALU.add)

            def gen_round(h, j):
                s_ = h % 2
                ks = (h * 8 + j) % 3
                for half in range(2):
                    bk = gen_bank()
                    for c in range(4):
                        mm(PS[bk], wkv_b[s_][:, c, 0:128], kvnT_all[:, c, half * 4:(half + 1) * 4, j, :], c == 0, c == 3,
                           [b_wkv[s_]] + b_kvall[half * 4:(half + 1) * 4], [PSb[bk]])
                    copy_any(KT[ks][:, half * 4:(half + 1) * 4, :], PS[bk].rearrange("p (r t) -> p r t", r=4), [PSb[bk]],
                             [b_KT[ks]])
                for half in range(2):
                    bk = gen_bank()
                    for rr in range(4):
                        r = half * 4 + rr
                        for c in range(4):
                            mm(PS[bk][:, rr * 128:(rr + 1) * 128], kvnT_all[:, c, r, j, :], wkv_b[s_][:, c, 128:256],
                               c == 0, c == 3, [b_wkv[s_], b_kvall[r]], [PSb[bk]])
                    copy_any(Vb[ks][:, half * 4:(half + 1) * 4, 0:128], PS[bk].rearrange("p (r t) -> p r t", r=4),
                             [PSb[bk]], [b_V[ks]])

            def attn_round(h, j):
                s_ = h % 2
                ks = (h * 8 + j) % 3
                q0 = j * 128
                nq = 1024 - q0
                chunks = [(q0, nq)] if nq <= 512 else [(q0, 512), (q0 + 512, nq - 512)]
                for r in range(8):
                    for (c0, n) in chunks:
                        sb_ = 3 + st_rr[0] % 3
                        pi = st_rr[0] % 3
                        st_rr[0] += 1
                        mm(PS[sb_][:, 0:n], KT[ks][:, r, :], qT[s_][:, c0:c0 + n], True, False,
                           [b_KT[ks], b_qT[s_]], [PSb[sb_]])
                        mm(PS[sb_][:, 0:n], kropeT_all[:, r, j, :], qrT[s_][:, c0:c0 + n], False, True,
                           [b_kvall[r], b_qT[s_]], [PSb[sb_]])
                        act(PT[pi][:, 0:n], PS[sb_][:, 0:n], AF.Exp, [PSb[sb_]], [b_PT[pi]])
                        if c0 == q0:
                            dve("tensor_tensor", [b_PT[pi], b_mask], [b_PT[pi]], out=PT[pi][:, 0:128], in0=PT[pi][:, 0:128],
                                in1=maskb[:, r, :], op=ALU.mult)
                        def pv(r=r, c0=c0, n=n, pi=pi):
                            for qb in range(c0 // 128, (c0 + n) // 128):
                                first = (j == 0 and r == 0 and OC[qb] == 0)
                                last = (j == qb and r == 7)
                                mm(Oap(qb, 0, 129), PT[pi][:, qb * 128 - c0:qb * 128 - c0 + 128], Vb[ks][:, r, 0:129],
                                   first, last, [b_PT[pi], b_V[ks]], [PSb[OB[qb]]], skip=True)
                        pend.append(pv)
                        flush(2)

                def epilogue():
                    qb = j
                    e_ = (h * 8 + j) % 2
                    act(osb[e_][:, 0:129], Oap(qb, 0, 129), AF.Copy, [PSb[OB[qb]]], [b_mtmp[e_]])
                    dve("reciprocal", [b_mtmp[e_]], [b_mtmp[e_]], out=rinv[:, e_:e_ + 1], in_=osb[e_][:, 128:129])
                    dve("scalar_tensor_tensor", [b_mtmp[e_], b_gmh[s_]], [b_mtmp[e_]], out=mtmp[e_],
                        in0=osb[e_][:, 0:128], scalar=rinv[:, e_:e_ + 1], in1=gmh[s_][:, qb, :], op0=ALU.mult,
                        op1=ALU.mult)
                    dve("tensor_tensor", [b_mtmp[e_], b_ysgh[s_]], [b_mtmp[e_]], out=mrgb[e_], in0=mtmp[e_],
                        in1=ysgh[s_][:, qb, :], op=ALU.add)
                    bk = gen_bank()
                    pb = PS[bk].bitcast(BF16)
                    tr(pb[:, 0:128], mrgb[e_], ident_b, [b_mtmp[e_], b_ident_b], [PSb[bk]])
                    act(mergedT[:, h, qb * 128:(qb + 1) * 128], pb[:, 0:128], AF.Copy, [PSb[bk]], [b_mT[qb]])
                pend.append(epilogue)
                if j == 7 and h + 1 < NH:
                    pend.append(lambda: load_gates(h + 1))

            pend = []

            def flush(keep):
                while len(pend) > keep:
                    pend.pop(0)()

            NH = DEBUG.get("nheads", H)
            steps = [(h, j) for h in range(NH) for j in range(8)]
            head_prologue(0)
            gen_round(0, 0)
            for i, (h, j) in enumerate(steps):
                if i + 1 < len(steps):
                    h2, j2 = steps[i + 1]
                    if j2 == 0:
                        head_prologue(h2)
                    gen_round(h2, j2)
                attn_round(h, j)
            flush(0)
            if "mergedT" in dbg_out:
                dma(dbg_out["mergedT"][:, :, :], mergedT, b_mT, [], "dbg")

            if STOP < 8:
                raise _Stop()
            xnew = R1.t[:, 0:16384].rearrange("p (b n) -> p b n", b=NB)
            gate_bc = R1.t[:, 16384:18432]
            fg_bc = R1.t[:, 18432:20480]
            b_xnew = [Buf("xnew%d" % b) for b in range(NB)]
            b_gate, b_fg = Buf("gate"), Buf("fg")
            fence(b_kvall, b_xnew + [b_gate, b_fg])
            dma(gate_bc, mod_d[0:1, 2 * D:3 * D].broadcast_to([128, D]), [b_modd], [b_gate], "c9")
            dma(fg_bc, fng_d[0:1, :].broadcast_to([128, D]), [], [b_fg], "c9")
            R3.reset()
            w_st = [R3.f32(2 * GW).rearrange("p (k n) -> p k n", k=2) for _ in range(NST)] + w_st_x
            w_bf = [R3.bf16(16 * GW).rearrange("p (k n) -> p k n", k=16) for _ in range(2)]
            fence([b_wqst, b_wkvst] + b_wq + b_wkv + b_qT + b_gmh + b_ysgh + b_KT + b_V,
                  b_wst + [q for l in b_wbf for q in l])
            xr = [R5.t[:, i * 512:(i + 1) * 512] for i in range(3)]
            otmp = [R5.t[:, 1536 + i * 512:1536 + (i + 1) * 512] for i in range(2)]
            b_xr = [Buf("xr%d" % i) for i in range(3)]
            b_otmp = [Buf("otmp%d" % i) for i in range(2)]
            fence(b_qnT, b_xr + b_otmp)
            sq2 = R6.t[:, 0:512]
            b_sq2 = Buf("sq2")
            fence(b_KT + b_V + b_PT + b_mtmp + [b_ropeA], [b_sq2] + b_wst_x)
            cnt = 0
            for n in range(4):
                wt, wb = load_weight_cg(wout_d, [(n * 512, 512)])
                for b in range(NB):
                    xi = cnt % 3
                    oi = cnt % 2
                    cnt += 1
                    dma(xr[xi], x_d[b * 128:(b + 1) * 128, n * 512:(n + 1) * 512], [], [b_xr[xi]], "xr%d" % xi)
                    bk = next_bank()
                    for k in range(16):
                        mm(PS[bk], mergedT[:, k, b * 128:(b + 1) * 128], wt[:, k, 0:512], k == 0, k == 15,
                           [b_mT[b], wb[k // 2]], [PSb[bk]])
                    dve("tensor_tensor", [PSb[bk], b_gate], [b_otmp[oi]], out=otmp[oi], in0=PS[bk],
                        in1=gate_bc[:, n * 512:(n + 1) * 512], op=ALU.mult)
                    dve("tensor_tensor", [b_otmp[oi], b_xr[xi]], [b_xnew[b]], out=xnew[:, b, n * 512:(n + 1) * 512],
                        in0=otmp[oi], in1=xr[xi], op=ALU.add)
                    act(sq2, xnew[:, b, n * 512:(n + 1) * 512], AF.Square, [b_xnew[b]], [b_sq2, b_stf[b]],
                        accum_out=st_f4[:, b, n:n + 1])
            for b in range(NB):
                dve("tensor_reduce", [b_stf[b]], [b_stf[b]], out=st_f[:, b:b + 1], in_=st_f4[:, b, :], axis=AX.X, op=ALU.add)
                rstd_from_ssq(st_f[:, b:b + 1], b_stf[b], D)
                dve("scalar_tensor_tensor", [b_xnew[b], b_stf[b], b_fg], [b_xnew[b]], out=xnew[:, b, :], in0=xnew[:, b, :],
                    scalar=st_f[:, b:b + 1], in1=fg_bc, op0=ALU.mult, op1=ALU.mult)
                dma(out_d[b * 128:(b + 1) * 128, :], xnew[:, b, :], [b_xnew[b]], [], "out%d" % (b % 2))
        except _Stop:
            pass
        P.emit(es)
    return nc, P


def _prep_inputs(inputs):
    f = lambda k: np.ascontiguousarray(np.asarray(inputs[k]), dtype=np.float32)
    x = f("x").reshape(S // 128, 128, D)
    pos = np.ascontiguousarray(np.asarray(inputs["positions"]), dtype=np.int32).reshape(S // 128, 128)
    c = f("c").reshape(16, 128).T.copy()
    w_uq = f("w_uq").reshape(QL, H, 192)
    nope = w_uq[:, :, 0:128]
    rope = w_uq[:, :, 128:192]
    ropesw = np.concatenate([w_uq[:, :, 160:192], w_uq[:, :, 128:160]], axis=2)
    wuq = np.ascontiguousarray(np.concatenate([nope, rope, rope, ropesw, ropesw], axis=2).transpose(1, 0, 2))
    wukv = np.ascontiguousarray(f("w_ukv").reshape(KVL, H, 256).transpose(1, 0, 2))
    wsT = np.ascontiguousarray(f("w_spatial").reshape(16, 128, 128).transpose(2, 0, 1))
    bspT = np.ascontiguousarray(f("b_spatial").reshape(16, 128).T)
    tri = np.triu(np.ones((128, 128), np.float32))
    ident = np.eye(128, dtype=np.float32)
    invf = (1.0 / (10000.0 ** (np.arange(0, 64, 2, dtype=np.float32) / np.float32(64)))).astype(np.float32)
    invf_bc = np.ascontiguousarray(np.broadcast_to(invf[None, :], (128, 32)))
    w_in = f("w_in").reshape(D, INW)
    cols = list(range(0, 1344)) + list(range(OFF_V, OFF_V + D))
    for g in range(H):
        for off in (OFF_U, OFF_ZS, OFF_GS, OFF_ZM, OFF_GM):
            cols += list(range(off + g * 128, off + (g + 1) * 128))
    w_in_p = np.ascontiguousarray(w_in[:, np.asarray(cols)])
    shared = {
        "cT": c, "w_ada": f("w_ada").reshape(D, 3 * D), "b_ada": f("b_ada").reshape(1, 3 * D),
        "attn_norm_g": f("attn_norm_g").reshape(1, D), "w_in": w_in_p,
        "q_norm_g": f("q_norm_g").reshape(1, QL), "kv_norm_g": f("kv_norm_g").reshape(1, KVL),
        "w_uq": wuq, "w_ukv": wukv, "sgu_norm_g": f("sgu_norm_g").reshape(1, D), "wsT": wsT, "bspT": bspT,
        "w_out": f("w_out").reshape(D, D), "final_norm_g": f("final_norm_g").reshape(1, D),
        "tri": tri, "ident": ident, "invf": invf_bc,
    }
    in_maps = []
    for i in range(NCORES):
        blocks = [8 * j + i for j in range(NB)]
        m = dict(shared)
        m["x"] = np.ascontiguousarray(x[blocks].reshape(TPC, D))
        m["pos"] = np.ascontiguousarray(pos[blocks].T)
        mask = np.zeros((128, 8, 128), np.float32)
        for r in range(8):
            if r < i:
                mask[:, r, :] = 1.0
            elif r == i:
                mask[:, r, :] = tri
        m["mask"] = mask
        in_maps.append(m)
    return in_maps


def kernel(**inputs):
    in_maps = _prep_inputs(inputs)
    dbg = DEBUG.get("dbg", ())
    nc, P = build_program(dbg)
    res = run_bass_kernel_spmd(nc, in_maps, core_ids=list(range(NCORES)))
    out = np.empty((S // 128, 128, D), np.float32)
    for i in range(NCORES):
        oc = np.asarray(res.results[i]["out"]).reshape(NB, 128, D)
        for j in range(NB):
            out[8 * j + i] = oc[j]
    if dbg:
        DEBUG["results"] = res.results
    return out.reshape(1, S, D)
```

```python
import math
from contextlib import ExitStack

import numpy as np
import concourse.bass as bass
import concourse.mybir as mybir
from concourse.bass_utils import run_bass_kernel_spmd

F32 = mybir.dt.float32
BF16 = mybir.dt.bfloat16
I32 = mybir.dt.int32
AF = mybir.ActivationFunctionType
ALU = mybir.AluOpType
AX = mybir.AxisListType

NCORES = 8
D = 2048
S = 8192
TPC = S // NCORES
NB = TPC // 128
H = 16
QL, KVL, RD = 768, 512, 64
INW = 13632
EPS = 1e-6
DEBUG = {}


class Buf:
    __slots__ = ("name", "w", "r")

    def __init__(self, name):
        self.name = name
        self.w = None
        self.r = {}


class Ins:
    __slots__ = ("eng", "fn", "deps", "slot", "sig", "need_sig", "idx")

    def __init__(self, eng, fn, slot):
        self.eng = eng
        self.fn = fn
        self.slot = slot
        self.deps = set()
        self.sig = None
        self.need_sig = False


class Prog:
    ENGS = ("pe", "act", "dve", "pool", "sp")

    def __init__(self, nc):
        self.nc = nc
        self.ins = []

    def op(self, eng, fn, reads=(), writes=(), slot=None):
        I = Ins(eng, fn, slot)
        I.idx = len(self.ins)
        key = eng if slot is None else ("dma", slot)
        psr = [b for b in reads if b.name.startswith("ps")]
        if psr:
            reads = [b for b in reads if not b.name.startswith("ps")]
            writes = list(writes) + [b for b in psr if b not in writes]
        deps = set()
        for b in reads:
            Dp = b.w
            if Dp is not None and Dp is not I:
                if not (Dp.slot is None and slot is None and Dp.eng == "pe" and eng == "pe"):
                    deps.add(Dp)
        for b in writes:
            Dp = b.w
            if Dp is not None and Dp is not I:
                if not (Dp.slot is None and slot is None and Dp.eng == eng):
                    deps.add(Dp)
            for R in b.r.values():
                if R is I:
                    continue
                if R.slot is None and slot is None and R.eng == eng:
                    continue
                deps.add(R)
        for b in reads:
            b.r[key] = I
        for b in writes:
            b.w = I
            b.r = {}
        I.deps = deps
        self.ins.append(I)
        return I

    def emit(self, es):
        nc = self.nc
        for I in self.ins:
            for Dp in I.deps:
                Dp.need_sig = True
        esem = {e: es.enter_context(nc.semaphore("sem_" + e)) for e in self.ENGS}
        slots = []
        seen = set()
        for I in self.ins:
            if I.slot is not None and I.slot not in seen:
                seen.add(I.slot)
                slots.append(I.slot)
        ssem = {s: es.enter_context(nc.semaphore("dsem_%d" % i)) for i, s in enumerate(slots)}
        ecnt = {e: 0 for e in self.ENGS}
        scnt = {s: 0 for s in slots}
        for I in self.ins:
            if I.slot is not None:
                inc = 1 if I.slot == "cc" else 16
                scnt[I.slot] += inc
                I.sig = (ssem[I.slot], scnt[I.slot], inc)
            elif I.need_sig:
                ecnt[I.eng] += 1
                I.sig = (esem[I.eng], ecnt[I.eng], 1)
        per_eng = {e: [] for e in self.ENGS}
        for I in self.ins:
            per_eng[I.eng].append(I)
        final = [(ssem[s], scnt[s]) for s in slots]
        self.stats = {e: len(per_eng[e]) for e in self.ENGS}
        self.stats["nsem"] = len(slots) + len(self.ENGS)
        self.stats["sig"] = dict(ecnt)

        def run_engine(ename, handle):
            waited = {}
            for I in per_eng[ename]:
                need = {}
                for Dp in I.deps:
                    sem, val, _ = Dp.sig
                    k = id(sem)
                    if waited.get(k, 0) >= val:
                        continue
                    if k not in need or need[k][1] < val:
                        need[k] = (sem, val)
                for k, (sem, val) in need.items():
                    handle.wait_ge(sem, val)
                    waited[k] = val
                bi = I.fn(handle)
                if I.sig is not None:
                    bi.then_inc(I.sig[0], I.sig[2])
            if ename == "sp":
                for sem, val in final:
                    if val > 0:
                        handle.wait_ge(sem, val)

        with nc.Block() as block:
            @block.tensor
            def _(h):
                run_engine("pe", h)

            @block.scalar
            def _(h):
                run_engine("act", h)

            @block.vector
            def _(h):
                run_engine("dve", h)

            @block.gpsimd
            def _(h):
                run_engine("pool", h)

            @block.sync
            def _(h):
                run_engine("sp", h)


PI = math.pi
QSCALE = 1.0 / math.sqrt(192.0)
GW = 640
OFF_Q, OFF_KV, OFF_KR, OFF_ZM, OFF_U, OFF_V, OFF_ZS, OFF_GM, OFF_GS = (
    0, 768, 1280, 1344, 3392, 5440, 7488, 9536, 11584)


class Region:
    def __init__(self, t, ncols):
        self.t = t
        self.n = ncols
        self.off = 0

    def reset(self, off=0):
        self.off = off

    def f32(self, ncols):
        assert self.off + ncols <= self.n, (self.off, ncols, self.n)
        ap = self.t[:, self.off:self.off + ncols]
        self.off += ncols
        return ap

    def bf16(self, nelem):
        nc32 = (nelem + 1) // 2
        return self.f32(nc32).bitcast(BF16)


class _Stop(Exception):
    pass


def build_program(dbg=()):
    STOP = DEBUG.get("stop", 99)
    nc = bass.Bass("TRN2", target_bir_lowering=False)
    P = Prog(nc)

    def din(name, shape, dt=F32):
        return nc.dram_tensor(name, list(shape), dt, kind="ExternalInput")

    x_d = din("x", [TPC, D])
    pos_d = din("pos", [128, NB], I32)
    c_d = din("cT", [128, 16])
    wada_d = din("w_ada", [D, 3 * D])
    bada_d = din("b_ada", [1, 3 * D])
    gattn_d = din("attn_norm_g", [1, D])
    win_d = din("w_in", [D, INW])
    qg_d = din("q_norm_g", [1, QL])
    kvg_d = din("kv_norm_g", [1, KVL])
    wuq_d = din("w_uq", [H, QL, 384])
    wukv_d = din("w_ukv", [H, KVL, 256])
    gv_d = din("sgu_norm_g", [1, D])
    wsT_d = din("wsT", [128, 16, 128])
    bspT_d = din("bspT", [128, 16])
    wout_d = din("w_out", [D, D])
    fng_d = din("final_norm_g", [1, D])
    mask_d = din("mask", [128, 8, 128])
    tri_d = din("tri", [128, 128])
    ident_d = din("ident", [128, 128])
    invf_d = din("invf", [128, 32])
    out_d = nc.dram_tensor("out", [TPC, D], F32, kind="ExternalOutput")
    mod_d = nc.dram_tensor("mod_scr", [1, 3 * D], F32)
    gm_d = nc.dram_tensor("gm_scr", [H, 128, NB, 128], F32)
    ysg_d = nc.dram_tensor("ysg_scr", [H, 128, NB, 128], F32)
    agin_d = nc.dram_tensor("ag_in", [576, TPC], BF16)
    agout_d = nc.dram_tensor("ag_out", [NCORES * 576, TPC], BF16)
    dbg_out = {}
    for name, shape, dt in dbg:
        dbg_out[name] = nc.dram_tensor("dbg_" + name, list(shape), dt, kind="ExternalOutput")

    es = ExitStack()
    with es:
        try:
            def sbt(name, ncols):
                return es.enter_context(nc.sbuf_tensor(name, [128, ncols], F32))

            R1 = Region(sbt("R1", 20480), 20480)
            R2 = Region(sbt("R2", 8192), 8192)
            R3 = Region(sbt("R3", 12800), 12800)
            R5 = Region(sbt("R5", 3072), 3072)
            R6 = Region(sbt("R6", 4096), 4096)
            R7 = Region(sbt("R7", 3584), 3584)
            psum = es.enter_context(nc.psum_tensor("ps", [128, 4096], F32))
            PS = [psum[:, k * 512:(k + 1) * 512] for k in range(8)]
            PSb = [Buf("ps%d" % k) for k in range(8)]
            ps_rr = [0]

            def next_bank(banks=range(8)):
                banks = list(banks)
                k = banks[ps_rr[0] % len(banks)]
                ps_rr[0] += 1
                return k

            def fence(olds, news, eng="pool"):
                olds = list(olds)
                P.op(eng, lambda e: e.memset(fscr[:, 0:1], 0.0), reads=olds, writes=olds + list(news) + [fscr_b])

            def dma(out, in_, reads, writes, slot, eng="sp"):
                return P.op(eng, lambda e: e.dma_start(out=out, in_=in_), reads=reads, writes=writes, slot=slot)

            def mm(out, lhsT, rhs, start, stop, reads, writes, skip=False):
                if skip:
                    return P.op("pe", lambda e: e.matmul(out, lhsT, rhs, start=start, stop=stop,
                                                         skip_group_check=True), reads=reads, writes=writes)
                return P.op("pe", lambda e: e.matmul(out, lhsT, rhs, start=start, stop=stop),
                            reads=reads, writes=writes)

            def tr(out, in_, ident, reads, writes):
                return P.op("pe", lambda e: e.transpose(out, in_, ident), reads=reads, writes=writes)

            def act(out, in_, func, reads, writes, **kw):
                return P.op("act", lambda e: e.activation(out=out, in_=in_, func=func, **kw), reads=reads, writes=writes)

            def dve(name, reads, writes, *a, **kw):
                return P.op("dve", lambda e: getattr(e, name)(*a, **kw), reads=reads, writes=writes)

            cp_rr = [0]

            def copy_any(out, in_, reads, writes, scale=None):
                cp_rr[0] += 1
                if scale is not None or cp_rr[0] % 2 == 0:
                    kw = {} if scale is None else {"scale": scale}
                    return act(out, in_, AF.Copy, reads, writes, **kw)
                return dve("tensor_copy", reads, writes, out=out, in_=in_)

            def rstd_from_ssq(st_ap, buf, n):
                act(st_ap, st_ap, AF.Sqrt, [buf], [buf], scale=1.0 / n, bias=EPS)
                dve("reciprocal", [buf], [buf], out=st_ap, in_=st_ap)

            ident_f = R7.f32(128)
            ident_b = R7.bf16(128)
            tri_b = R7.bf16(128)
            maskb = R7.bf16(1024).rearrange("p (r q) -> p r q", r=8)
            bspT = R7.f32(16)
            invf = R7.f32(32)
            pos_i = R7.f32(8).bitcast(I32)
            pos_f = R7.f32(8)
            cos_t = R7.f32(256).rearrange("p (b d) -> p b d", b=8)
            sin_t = R7.f32(256).rearrange("p (b d) -> p b d", b=8)
            cs1T = R7.f32(1024)
            cs2T = R7.f32(1024)
            cT = R7.f32(16)
            csig = R7.f32(16)
            cact = R7.f32(16)
            st_x = R7.f32(8)
            st_q = R7.f32(8)
            st_kv = R7.f32(8)
            st_v4 = R7.f32(32).rearrange("p (b n) -> p b n", b=8)
            st_v = R7.f32(8)
            st_f4 = R7.f32(32).rearrange("p (b n) -> p b n", b=8)
            st_f = R7.f32(8)
            rinv = R7.f32(16)
            fscr = R7.f32(4)
            fscr_b = Buf("fscr")
            b_ident_f, b_ident_b, b_tri, b_mask, b_bsp, b_invf = (Buf(n) for n in
                                                                  ("identf", "identb", "tri", "mask", "bsp", "invf"))
            b_pos, b_cos, b_sin, b_cs1, b_cs2, b_c = (Buf(n) for n in ("pos", "cos", "sin", "cs1", "cs2", "c"))
            b_stx = [Buf("stx%d" % b) for b in range(NB)]
            b_stq = [Buf("stq%d" % b) for b in range(NB)]
            b_stkv = [Buf("stkv%d" % b) for b in range(NB)]
            b_stv = [Buf("stv%d" % b) for b in range(NB)]
            b_stf = [Buf("stf%d" % b) for b in range(NB)]

            R1.reset()
            mask_st = R1.f32(1024).rearrange("p (r q) -> p r q", r=8)
            tri_st = R1.f32(128)
            b_tmp0 = Buf("tmp0")
            dma(ident_f, ident_d[:, :], [], [b_ident_f], "c0")
            dma(mask_st, mask_d[:, :, :], [], [b_tmp0], "c1_11")
            dma(tri_st, tri_d[:, :], [], [b_tmp0], "c1_12")
            dma(bspT, bspT_d[:, :], [], [b_bsp], "c2_1")
            dma(invf, invf_d[:, :], [], [b_invf], "c2_2")
            dma(pos_i, pos_d[:, :], [], [b_pos], "c2_3")
            dma(cT, c_d[:, :], [], [b_c], "c2_4")
            dve("tensor_copy", [b_ident_f], [b_ident_b], out=ident_b, in_=ident_f)
            dve("tensor_copy", [b_tmp0], [b_mask], out=maskb, in_=mask_st)
            dve("tensor_copy", [b_tmp0], [b_tri], out=tri_b, in_=tri_st)
            dve("tensor_copy", [b_pos], [b_pos], out=pos_f, in_=pos_i)

            act(csig, cT, AF.Sigmoid, [b_c], [b_c])
            dve("tensor_tensor", [b_c], [b_c], out=cact, in0=cT, in1=csig, op=ALU.mult)
            R1.reset(1152)
            wa_st = [R1.f32(8192).rearrange("p (k n) -> p k n", k=16) for _ in range(2)]
            b_wa = [Buf("wa%d" % i) for i in range(2)]
            R2.reset()
            modrow = R2.f32(8192)[0:1, 0:6144]
            b_modrow = Buf("modrow")
            R3.reset()
            badarow = R3.f32(6144)[0:1, :]
            b_bada = Buf("bada")
            dma(badarow, bada_d[0:1, :], [], [b_bada], "c3")
            wada_v = wada_d.ap().rearrange("(k p) n -> p k n", p=128)
            for n in range(12):
                s_ = n % 2
                for hh in range(2):
                    dma(wa_st[s_][:, hh * 8:(hh + 1) * 8, :], wada_v[:, hh * 8:(hh + 1) * 8, n * 512:(n + 1) * 512],
                        [], [b_wa[s_]], "wa%d" % s_)
                bk = next_bank()
                for k in range(16):
                    mm(PS[bk][0:1, :], cact[:, k:k + 1], wa_st[s_][:, k, :], k == 0, k == 15,
                       [b_c, b_wa[s_]], [PSb[bk]])
                dve("tensor_tensor", [PSb[bk], b_bada], [b_modrow], out=modrow[:, n * 512:(n + 1) * 512],
                    in0=PS[bk][0:1, :], in1=badarow[:, n * 512:(n + 1) * 512], op=ALU.add)
            b_modd = Buf("mod_d")
            dma(mod_d[0:1, :], modrow, [b_modrow], [b_modd], "c4")

            R3.reset(6144)
            ang = R3.f32(256).rearrange("p (b d) -> p b d", b=8)
            sarg = R3.f32(256)
            carg = R3.f32(256)
            T1 = R3.f32(1024).rearrange("p (b d) -> p b d", b=8)
            T2 = R3.f32(1024).rearrange("p (b d) -> p b d", b=8)
            b_ang, b_T = Buf("ang"), Buf("T12")
            for b in range(NB):
                dve("tensor_scalar", [b_invf, b_pos], [b_ang], out=ang[:, b, :], in0=invf, scalar1=pos_f[:, b:b + 1],
                    scalar2=None, op0=ALU.mult)
            angf = ang.rearrange("p b d -> p (b d)")
            MAGIC = 12582912.0
            C1 = 6.28125
            C2 = 2.0 * PI - 6.28125
            nfs = R3.f32(256)
            for (dst, dbuf, r_ap, phase) in ((sin_t, b_sin, sarg, 0.0), (cos_t, b_cos, carg, 0.25)):
                dve("tensor_scalar", [b_ang], [b_ang], out=nfs, in0=angf, scalar1=1.0 / (2.0 * PI), scalar2=phase,
                    op0=ALU.mult, op1=ALU.add)
                dve("tensor_scalar", [b_ang], [b_ang], out=nfs, in0=nfs, scalar1=MAGIC, scalar2=None, op0=ALU.add)
                dve("tensor_scalar", [b_ang], [b_ang], out=nfs, in0=nfs, scalar1=-MAGIC, scalar2=None, op0=ALU.add)
                dve("scalar_tensor_tensor", [b_ang], [b_ang], out=r_ap, in0=nfs, scalar=-C1, in1=angf,
                    op0=ALU.mult, op1=ALU.add)
                dve("scalar_tensor_tensor", [b_ang], [b_ang], out=r_ap, in0=nfs, scalar=-C2, in1=r_ap,
                    op0=ALU.mult, op1=ALU.add)
                dve("tensor_scalar", [b_ang], [b_ang], out=r_ap, in0=r_ap, scalar1=phase * 2.0 * PI, scalar2=-PI,
                    op0=ALU.add, op1=ALU.max)
                dve("tensor_scalar", [b_ang], [b_ang], out=r_ap, in0=r_ap, scalar1=PI, scalar2=None, op0=ALU.min)
                act(dst.rearrange("p b d -> p (b d)"), r_ap, AF.Sin, [b_ang], [dbuf])
            for rr in range(4):
                dve("tensor_copy", [b_cos], [b_T], out=T1[:, :, rr * 32:(rr + 1) * 32], in_=cos_t)
                if rr % 2 == 0:
                    dve("tensor_scalar", [b_sin], [b_T], out=T2[:, :, rr * 32:(rr + 1) * 32], in0=sin_t, scalar1=-1.0,
                        scalar2=None, op0=ALU.mult)
                else:
                    dve("tensor_copy", [b_sin], [b_T], out=T2[:, :, rr * 32:(rr + 1) * 32], in_=sin_t)
            for (Tt, dst, bd) in ((T1, cs1T, b_cs1), (T2, cs2T, b_cs2)):
                for half in range(2):
                    bk = next_bank()
                    for bb in range(4):
                        b = half * 4 + bb
                        tr(PS[bk][:, bb * 128:(bb + 1) * 128], Tt[:, b, :], ident_f, [b_T, b_ident_f], [PSb[bk]])
                    act(dst[:, half * 512:(half + 1) * 512], PS[bk][:, :], AF.Copy, [PSb[bk]], [bd], scale=QSCALE)

            R1.reset()
            x_st = [R1.f32(2048) for _ in range(2)]
            b_xst = [Buf("xst%d" % i) for i in range(2)]
            hf = R1.f32(2048)
            hb = R1.bf16(2048)
            junk = R1.f32(2048)
            A_bc = R1.f32(2048)
            shift_bc = R1.f32(2048)
            gattn_bc = R1.f32(2048)
            b_hf, b_hb, b_junk, b_A, b_shift, b_gattn = (Buf(n) for n in ("hf", "hb", "junk", "A", "shift", "gattn"))
            fence([b_wa[0], b_wa[1], b_tmp0], [b_xst[0], b_xst[1], b_hf, b_hb, b_junk, b_A, b_shift, b_gattn])
            dma(A_bc, mod_d[0:1, D:2 * D].broadcast_to([128, D]), [b_modd], [b_A], "c5_13")
            dma(shift_bc, mod_d[0:1, 0:D].broadcast_to([128, D]), [b_modd], [b_shift], "c5_14")
            dma(gattn_bc, gattn_d[0:1, :].broadcast_to([128, D]), [], [b_gattn], "c5_15")
            dve("scalar_tensor_tensor", [b_A, b_gattn], [b_A], out=A_bc, in0=A_bc, scalar=1.0, in1=gattn_bc,
                op0=ALU.add, op1=ALU.mult)
            hT = R2.t[:, 0:8192].bitcast(BF16).rearrange("p (k t) -> p k t", k=16)
            b_hT = [Buf("hT%d" % b) for b in range(NB)]
            fence([b_modrow], b_hT)
            for b in range(NB):
                s_ = b % 2
                dma(x_st[s_], x_d[b * 128:(b + 1) * 128, :], [], [b_xst[s_]], "xst%d" % s_)
                act(junk, x_st[s_], AF.Square, [b_xst[s_]], [b_junk, b_stx[b]], accum_out=st_x[:, b:b + 1])
                rstd_from_ssq(st_x[:, b:b + 1], b_stx[b], D)
                dve("scalar_tensor_tensor", [b_xst[s_], b_stx[b], b_A], [b_hf], out=hf, in0=x_st[s_],
                    scalar=st_x[:, b:b + 1], in1=A_bc, op0=ALU.mult, op1=ALU.mult)
                dve("tensor_tensor", [b_hf, b_shift], [b_hb], out=hb, in0=hf, in1=shift_bc, op=ALU.add)
                for half in range(2):
                    bk = next_bank()
                    pb = PS[bk].bitcast(BF16)
                    for kk in range(8):
                        k = half * 8 + kk
                        tr(pb[:, kk * 128:(kk + 1) * 128], hb[:, k * 128:(k + 1) * 128], ident_b,
                           [b_hb, b_ident_b], [PSb[bk]])
                    copy_any(hT[:, half * 8:(half + 1) * 8, b * 128:(b + 1) * 128],
                             pb.rearrange("p (k t) -> p k t", k=8), [PSb[bk]], [b_hT[b]])

            if STOP < 4:
                raise _Stop()
            R3.reset()
            NST = 2
            w_st = [R3.f32(2 * GW).rearrange("p (k n) -> p k n", k=2) for _ in range(NST)]
            b_wst = [Buf("wst%d" % i) for i in range(NST)]
            w_bf = [R3.bf16(16 * GW).rearrange("p (k n) -> p k n", k=16) for _ in range(2)]
            b_wbf = [[Buf("wbf%d_%d" % (i, q)) for q in range(8)] for i in range(2)]
            fence([b_bada, b_ang, b_T], b_wst + [q for l in b_wbf for q in l])
            wst_rr = [0]
            nst_active = [NST]
            w_st_x = [R6.t[:, 1024 + i * 2 * GW:1024 + (i + 1) * 2 * GW].rearrange("p (k n) -> p k n", k=2)
                      for i in range(2)]
            b_wst_x = [Buf("wst%d" % (NST + i)) for i in range(2)]
            wbf_rr = [0]

            def load_weight_cg(src, pieces, lazy=False):
                slot = wbf_rr[0] % 2
                wbf_rr[0] += 1
                wtot = sum(w for _, w in pieces)
                v = src.ap().rearrange("(k p) n -> p k n", p=128)
                def piece(q):
                    ss = wst_rr[0] % nst_active[0]
                    wst_rr[0] += 1
                    c = 0
                    for (c0, w) in pieces:
                        dma(w_st[ss][:, :, c:c + w], v[:, 2 * q:2 * q + 2, c0:c0 + w], [], [b_wst[ss]], "wst%d" % ss)
                        c += w
                    copy_any(w_bf[slot][:, 2 * q:2 * q + 2, 0:wtot], w_st[ss][:, :, 0:wtot], [b_wst[ss]],
                             [b_wbf[slot][q]])
                if lazy:
                    return w_bf[slot], b_wbf[slot], piece
                for q in range(8):
                    piece(q)
                return w_bf[slot], b_wbf[slot]

            R1.reset()
            lat = R1.f32(NB * 1344).rearrange("p (b n) -> p b n", b=NB)
            b_lat = [Buf("lat%d" % b) for b in range(NB)]
            v_bf = R1.bf16(NB * 2048).rearrange("p (b n) -> p b n", b=NB)
            b_vbf = [Buf("vbf%d" % b) for b in range(NB)]
            qg_bc = R1.f32(768)
            kvg_bc = R1.f32(512)
            b_qg, b_kvg = Buf("qg"), Buf("kvg")
            fence([b_xst[0], b_xst[1], b_hf, b_hb, b_junk, b_A, b_shift, b_gattn], b_lat + b_vbf + [b_qg, b_kvg])
            dma(qg_bc, qg_d[0:1, :].broadcast_to([128, QL]), [], [b_qg], "c6_5")
            dma(kvg_bc, kvg_d[0:1, :].broadcast_to([128, KVL]), [], [b_kvg], "c6_6")
            qnT = R5.t[:, 0:3072].bitcast(BF16).rearrange("p (k t) -> p k t", k=6)
            b_qnT = [Buf("qnT%d" % b) for b in range(NB)]
            R6.reset()
            kvnT_own = R6.bf16(4 * 1024).rearrange("p (k t) -> p k t", k=4)
            kropeT_own = R6.bf16(1024)
            b_kvown = [Buf("kvown%d" % b) for b in range(NB)]
            qn_b = R6.bf16(768)
            kvn_b = R6.bf16(512)
            krot = R6.bf16(64)
            rt = [R6.f32(32) for _ in range(4)]
            sqj = R6.bf16(768)
            b_qnb, b_kvnb, b_krot, b_rt, b_sqj = Buf("qnb"), Buf("kvnb"), Buf("krot"), Buf("rt"), Buf("sqj")

            def inproj_cg(wt, wb, width, evac):
                for b in range(NB):
                    bk = next_bank()
                    for k in range(16):
                        mm(PS[bk][:, 0:width], hT[:, k, b * 128:(b + 1) * 128], wt[:, k, 0:width], k == 0, k == 15,
                           [b_hT[b], wb[k // 2]], [PSb[bk]])
                    evac(b, bk)

            for (c0, w) in ((0, 512), (512, 512), (1024, 320)):
                wt, wb = load_weight_cg(win_d, [(c0, w)])

                def evac_lat(b, bk, c0=c0, w=w):
                    copy_any(lat[:, b, c0:c0 + w], PS[bk][:, 0:w], [PSb[bk]], [b_lat[b]])
                inproj_cg(wt, wb, w, evac_lat)

            def lat_post(b):
                act(sqj, lat[:, b, 0:768], AF.Square, [b_lat[b]], [b_sqj, b_stq[b]], accum_out=st_q[:, b:b + 1])
                rstd_from_ssq(st_q[:, b:b + 1], b_stq[b], QL)
                dve("scalar_tensor_tensor", [b_lat[b], b_stq[b], b_qg], [b_qnb], out=qn_b, in0=lat[:, b, 0:768],
                    scalar=st_q[:, b:b + 1], in1=qg_bc, op0=ALU.mult, op1=ALU.mult)
                bk = next_bank()
                pb = PS[bk].bitcast(BF16)
                for k in range(6):
                    tr(pb[:, k * 128:(k + 1) * 128], qn_b[:, k * 128:(k + 1) * 128], ident_b, [b_qnb, b_ident_b],
                       [PSb[bk]])
                copy_any(qnT[:, :, b * 128:(b + 1) * 128], pb[:, 0:768].rearrange("p (k t) -> p k t", k=6),
                         [PSb[bk]], [b_qnT[b]])
                act(sqj[:, 0:512], lat[:, b, 768:1280], AF.Square, [b_lat[b]], [b_sqj, b_stkv[b]],
                    accum_out=st_kv[:, b:b + 1])
                rstd_from_ssq(st_kv[:, b:b + 1], b_stkv[b], KVL)
                dve("scalar_tensor_tensor", [b_lat[b], b_stkv[b], b_kvg], [b_kvnb], out=kvn_b, in0=lat[:, b, 768:1280],
                    scalar=st_kv[:, b:b + 1], in1=kvg_bc, op0=ALU.mult, op1=ALU.mult)
                t1 = lat[:, b, 1280:1312]
                t2 = lat[:, b, 1312:1344]
                dve("tensor_tensor", [b_lat[b], b_cos], [b_rt], out=rt[0], in0=t1, in1=cos_t[:, b, :], op=ALU.mult)
                dve("tensor_tensor", [b_lat[b], b_sin], [b_rt], out=rt[1], in0=t2, in1=sin_t[:, b, :], op=ALU.mult)
                dve("tensor_tensor", [b_lat[b], b_cos], [b_rt], out=rt[2], in0=t2, in1=cos_t[:, b, :], op=ALU.mult)
                dve("tensor_tensor", [b_lat[b], b_sin], [b_rt], out=rt[3], in0=t1, in1=sin_t[:, b, :], op=ALU.mult)
                dve("tensor_tensor", [b_rt], [b_krot], out=krot[:, 0:32], in0=rt[0], in1=rt[1], op=ALU.subtract)
                dve("tensor_tensor", [b_rt], [b_krot], out=krot[:, 32:64], in0=rt[2], in1=rt[3], op=ALU.add)
                bk = next_bank()
                pb = PS[bk].bitcast(BF16)
                for k in range(4):
                    tr(pb[:, k * 128:(k + 1) * 128], kvn_b[:, k * 128:(k + 1) * 128], ident_b, [b_kvnb, b_ident_b],
                       [PSb[bk]])
                tr(pb[0:64, 512:640], krot[:, 0:64], ident_b, [b_krot, b_ident_b], [PSb[bk]])
                copy_any(kvnT_own[:, :, b * 128:(b + 1) * 128], pb[:, 0:512].rearrange("p (k t) -> p k t", k=4),
                         [PSb[bk]], [b_kvown[b]])
                copy_any(kropeT_own[0:64, b * 128:(b + 1) * 128], pb[0:64, 512:640], [PSb[bk]], [b_kvown[b]])

            def evac_v_factory(n):
                def evac_v(b, bk):
                    act(sqj[:, 0:512], PS[bk][:, 0:512], AF.Square, [PSb[bk]], [b_sqj, b_stv[b]],
                        accum_out=st_v4[:, b, n:n + 1])
                    dve("tensor_copy", [PSb[bk]], [b_vbf[b]], out=v_bf[:, b, n * 512:(n + 1) * 512], in_=PS[bk][:, 0:512])
                return evac_v

            for n in range(4):
                wt, wb = load_weight_cg(win_d, [(1344 + n * 512, 512)])
                inproj_cg(wt, wb, 512, evac_v_factory(n))
                if n < 2:
                    for b in range(n * 4, n * 4 + 4):
                        lat_post(b)
                if n == 1:
                    b_agin, b_agout = Buf("agin"), Buf("agout")
                    dma(agin_d[0:512, :].rearrange("(k p) t -> p k t", p=128), kvnT_own, b_kvown, [b_agin], "c7")
                    dma(agin_d[512:576, :], kropeT_own[0:64, :], b_kvown, [b_agin], "c7")
                    P.op("pool", lambda e: e.collective_compute("AllGather", ALU.bypass,
                                                                replica_groups=[list(range(NCORES))],
                                                                ins=[agin_d.ap().opt()], outs=[agout_d.ap().opt()]),
                         reads=[b_agin], writes=[b_agout], slot="cc")
            if "qnT" in dbg_out:
                dma(dbg_out["qnT"][:, :, :], qnT, b_qnT, [], "dbg")
            if "agout" in dbg_out:
                dma(dbg_out["agout"][:, :], agout_d[:, :], [b_agout], [], "dbg")

            if STOP < 5:
                raise _Stop()
            R1.reset()
            wsT_st = R1.f32(2048).rearrange("p (g t) -> p g t", g=16)
            gv_bc = R1.f32(2048)
            NT = 2
            sgs = [R1.f32(256) for _ in range(NT)]
            sgm = [R1.f32(256) for _ in range(NT)]
            ub = [R1.f32(128) for _ in range(NT)]
            tt1 = [R1.f32(128) for _ in range(NT)]
            tt2 = [R1.f32(128) for _ in range(NT)]
            tm = [R1.f32(128) for _ in range(NT)]
            ta = [R1.f32(128) for _ in range(NT)]
            gm_st = [R1.f32(1024).rearrange("p (b n) -> p b n", b=NB) for _ in range(2)]
            ysg_st = [R1.f32(1024).rearrange("p (b n) -> p b n", b=NB) for _ in range(2)]
            b_wsst, b_gv = Buf("wsst"), Buf("gv")
            b_t = [Buf("dtmp%d" % i) for i in range(NT)]
            b_gmst = [Buf("gmst%d" % i) for i in range(2)]
            b_ysgst = [Buf("ysgst%d" % i) for i in range(2)]
            fence(b_lat, [b_wsst, b_gv] + b_t + b_gmst + b_ysgst)
            wsT_b = R6.t[:, 0:1024].bitcast(BF16).rearrange("p (g t) -> p g t", g=16)
            b_ws = Buf("wsTb")
            fence(b_kvown + [b_qnb, b_kvnb, b_krot, b_rt], [b_ws])
            fence(b_kvown + [b_qnb, b_kvnb, b_krot, b_rt, b_sqj], b_wst_x)
            w_st = w_st + w_st_x
            b_wst = b_wst + b_wst_x
            nst_active[0] = NST + 2
            dma(wsT_st, wsT_d[:, :, :], [], [b_wsst], "c8_7")
            dma(gv_bc, gv_d[0:1, :].broadcast_to([128, D]), [], [b_gv], "c8_8")
            for g in range(16):
                dve("tensor_tensor", [b_wsst, b_tri], [b_ws], out=wsT_b[:, g, :], in0=wsT_st[:, g, :], in1=tri_b,
                    op=ALU.mult)
            for b in range(NB):
                dve("tensor_reduce", [b_stv[b]], [b_stv[b]], out=st_v[:, b:b + 1], in_=st_v4[:, b, :], axis=AX.X,
                    op=ALU.add)
                rstd_from_ssq(st_v[:, b:b + 1], b_stv[b], D)
                dve("scalar_tensor_tensor", [b_vbf[b], b_stv[b], b_gv], [b_vbf[b]], out=v_bf[:, b, :], in0=v_bf[:, b, :],
                    scalar=st_v[:, b:b + 1], in1=gv_bc, op0=ALU.mult, op1=ALU.mult)
            b_gmd = [Buf("gmd%d" % g) for g in range(H)]
            b_ysgd = [Buf("ysgd%d" % g) for g in range(H)]
            tcnt = [0]
            nxt = load_weight_cg(win_d, [(3392, GW)])
            for g in range(H):
                o = g * 128
                wt, wb = nxt[0], nxt[1]
                nxt = load_weight_cg(win_d, [(3392 + (g + 1) * GW, GW)], lazy=True) if g + 1 < H else None
                so = g % 2
                for b in range(NB):
                    bx = next_bank()
                    by = next_bank()
                    for k in range(16):
                        mm(PS[bx][:, 0:384], hT[:, k, b * 128:(b + 1) * 128], wt[:, k, 0:384], k == 0, k == 15,
                           [b_hT[b], wb[k // 2]], [PSb[bx]])
                        mm(PS[by][:, 0:256], hT[:, k, b * 128:(b + 1) * 128], wt[:, k, 384:640], k == 0, k == 15,
                           [b_hT[b], wb[k // 2]], [PSb[by]])
                    mm(PS[by][:, 256:384], wsT_b[:, g, :], v_bf[:, b, o:o + 128], True, True, [b_ws, b_vbf[b]], [PSb[by]])
                    ti = tcnt[0] % NT
                    tcnt[0] += 1
                    bt = b_t[ti]
                    act(sgs[ti], PS[bx][:, 128:384], AF.Sigmoid, [PSb[bx]], [bt])
                    act(sgm[ti], PS[by][:, 0:256], AF.Sigmoid, [PSb[by]], [bt])
                    act(ub[ti], PS[bx][:, 0:128], AF.Copy, [PSb[bx]], [bt])
                    dve("tensor_tensor", [PSb[bx], bt], [bt], out=tt1[ti], in0=PS[bx][:, 128:256], in1=sgs[ti][:, 0:128],
                        op=ALU.mult)
                    dve("tensor_tensor", [bt], [bt], out=tt2[ti], in0=tt1[ti], in1=sgs[ti][:, 128:256], op=ALU.mult)
                    dve("scalar_tensor_tensor", [PSb[by], bt, b_bsp], [bt], out=tm[ti], in0=PS[by][:, 256:384],
                        scalar=bspT[:, g:g + 1], in1=ub[ti], op0=ALU.add, op1=ALU.mult)
                    dve("tensor_tensor", [bt], [b_ysgst[so]], out=ysg_st[so][:, b, :], in0=tm[ti], in1=tt2[ti],
                        op=ALU.mult)
                    dve("tensor_tensor", [PSb[by], bt], [bt], out=ta[ti], in0=PS[by][:, 0:128], in1=sgm[ti][:, 0:128],
                        op=ALU.mult)
                    dve("tensor_tensor", [bt], [b_gmst[so]], out=gm_st[so][:, b, :], in0=ta[ti], in1=sgm[ti][:, 128:256],
                        op=ALU.mult)
                    if nxt is not None:
                        nxt[2](b)
                dma(gm_d[g], gm_st[so], [b_gmst[so]], [b_gmd[g]], "gmst%d" % so)
                dma(ysg_d[g], ysg_st[so], [b_ysgst[so]], [b_ysgd[g]], "ysgst%d" % so)
            if "gm" in dbg_out:
                dma(dbg_out["gm"][:, :, :, :], gm_d[:, :, :, :], b_gmd, [], "dbg")
                dma(dbg_out["ysg"][:, :, :, :], ysg_d[:, :, :, :], b_ysgd, [], "dbg")

            if STOP < 6:
                raise _Stop()
            kvnT_all = R1.t[:, 0:16384].bitcast(BF16).rearrange("p (c r j t) -> p c r j t", c=4, r=8, j=8)
            kropeT_all = R1.t[:, 16384:20480].bitcast(BF16).rearrange("p (r j t) -> p r j t", r=8, j=8)
            b_kvall = [Buf("kvall%d" % r) for r in range(NCORES)]
            fence(b_vbf + [b_wsst, b_gv, b_qg, b_kvg] + b_t + b_gmst + b_ysgst, b_kvall)
            P.op("dve", lambda e: e.memset(kropeT_all[64:128, :, :, :], 0.0), reads=[], writes=b_kvall)
            for r in range(NCORES):
                dma(kvnT_all[:, :, r, :, :].rearrange("p c j t -> p c (j t)"),
                    agout_d[r * 576:r * 576 + 512, :].rearrange("(c p) t -> p c t", p=128), [b_agout], [b_kvall[r]],
                    "kvall")
                dma(kropeT_all[0:64, r, :, :].rearrange("p j t -> p (j t)"), agout_d[r * 576 + 512:(r + 1) * 576, :],
                    [b_agout], [b_kvall[r]], "kvall")

            mergedT = R2.t[:, 0:8192].bitcast(BF16).rearrange("p (k t) -> p k t", k=16)
            b_mT = [Buf("mT%d" % b) for b in range(NB)]
            fence(b_hT, b_mT)
            R3.reset()
            wq_st = R3.f32(6 * 384).rearrange("p (k n) -> p k n", k=6)
            wkv_st = R3.f32(4 * 256).rearrange("p (k n) -> p k n", k=4)
            wq_b = [R3.bf16(6 * 384).rearrange("p (k n) -> p k n", k=6) for _ in range(2)]
            wkv_b = [R3.bf16(4 * 256).rearrange("p (k n) -> p k n", k=4) for _ in range(2)]
            qT = [R3.bf16(1024) for _ in range(2)]
            qrT = [R3.bf16(1024) for _ in range(2)]
            gmh = [R3.f32(1024).rearrange("p (b n) -> p b n", b=NB)] * 2
            ysgh = [R3.f32(1024).rearrange("p (b n) -> p b n", b=NB)] * 2
            KT2 = R3.bf16(1024).rearrange("p (r t) -> p r t", r=8)
            Vb2 = R3.bf16(8 * 130).rearrange("p (r n) -> p r n", r=8)
            R6.reset()
            KT = [R6.bf16(1024).rearrange("p (r t) -> p r t", r=8) for _ in range(2)]
            Vb = [R6.bf16(8 * 130).rearrange("p (r n) -> p r n", r=8) for _ in range(2)]
            KT.append(KT2)
            Vb.append(Vb2)
            PT = [R6.bf16(512) for _ in range(3)]
            mtmp = [R6.f32(128) for _ in range(2)]
            mrgb = [R6.bf16(128) for _ in range(2)]
            ropeA = R6.f32(512)
            osb = [R6.f32(130) for _ in range(2)]
            b_wqst, b_wkvst = Buf("wqst"), Buf("wkvst")
            b_wq = [Buf("wq%d" % i) for i in range(2)]
            b_wkv = [Buf("wkv%d" % i) for i in range(2)]
            b_qT = [Buf("qT%d" % i) for i in range(2)]
            b_gmh = [Buf("gmh0")] * 2
            b_ysgh = [Buf("ysgh0")] * 2
            b_KT = [Buf("KT%d" % i) for i in range(3)]
            b_V = [Buf("V%d" % i) for i in range(3)]
            b_PT = [Buf("PT%d" % i) for i in range(3)]
            b_mtmp = [Buf("mtmp%d" % i) for i in range(2)]
            b_ropeA = Buf("ropeA")
            b_O = [Buf("O%d" % i) for i in range(NB)]
            fence(b_wst + [q for l in b_wbf for q in l],
                  [b_wqst, b_wkvst] + b_wq + b_wkv + b_qT + b_gmh + b_ysgh)
            fence([b_ws] + b_wst_x, b_KT + b_V + b_PT + b_mtmp + [b_ropeA])
            fence(b_wst + [q for l in b_wbf for q in l], [b_KT[2], b_V[2]])
            for s_ in range(3):
                P.op("dve", lambda e, s_=s_: e.memset(Vb[s_][:, :, 128:130], 1.0), reads=[], writes=[b_V[s_]])
            OB = [0, 0, 0, 1, 1, 1, 2, 2]
            OC = [0, 129, 258, 0, 129, 258, 0, 129]

            def Oap(qb, lo, hi):
                return PS[OB[qb]][:, OC[qb] + lo:OC[qb] + hi]
            st_rr = [0]
            gen_rr = [0]

            def gen_bank():
                gen_rr[0] += 1
                return 6 + gen_rr[0] % 2

            def load_gates(h):
                dma(gmh[0], gm_d[h], [b_gmd[h]], [b_gmh[0]], "gmh0")
                dma(ysgh[0], ysg_d[h], [b_ysgd[h]], [b_ysgh[0]], "ysgh0")

            def head_prologue(h):
                s_ = h % 2
                dma(wq_st, wuq_d[h].rearrange("(k p) n -> p k n", p=128), [], [b_wqst], "wqst")
                dma(wkv_st, wukv_d[h].rearrange("(k p) n -> p k n", p=128), [], [b_wkvst], "wkvst")
                copy_any(wq_b[s_], wq_st, [b_wqst], [b_wq[s_]])
                copy_any(wkv_b[s_], wkv_st, [b_wkvst], [b_wkv[s_]])
                if h == 0:
                    load_gates(0)
                for half in range(2):
                    tk = slice(half * 512, (half + 1) * 512)
                    bk = gen_bank()
                    for k in range(6):
                        mm(PS[bk], wq_b[s_][:, k, 0:128], qnT[:, k, tk], k == 0, k == 5, [b_wq[s_]] + b_qnT, [PSb[bk]])
                    act(qT[s_][:, tk], PS[bk], AF.Copy, [PSb[bk]], [b_qT[s_]], scale=QSCALE)
                    bk = gen_bank()
                    for k in range(6):
                        mm(PS[bk], wq_b[s_][:, k, 128:256], qnT[:, k, tk], k == 0, k == 5, [b_wq[s_]] + b_qnT, [PSb[bk]])
                    dve("tensor_tensor", [PSb[bk], b_cs1], [b_ropeA], out=ropeA, in0=PS[bk], in1=cs1T[:, tk], op=ALU.mult)
                    bk2 = gen_bank()
                    for k in range(6):
                        mm(PS[bk2], wq_b[s_][:, k, 256:384], qnT[:, k, tk], k == 0, k == 5, [b_wq[s_]] + b_qnT, [PSb[bk2]])
                    dve("tensor_tensor", [PSb[bk2], b_cs2], [PSb[bk2]], out=PS[bk2], in0=PS[bk2], in1=cs2T[:, tk],
                        op=ALU.mult)
                    dve("tensor_tensor", [PSb[bk2], b_ropeA], [b_qT[s_]], out=qrT[s_][:, tk], in0=PS[bk2], in1=ropeA,
                        op=ALU.add)

            def gen_round(h, j):
                s_ = h % 2
                ks = (h * 8 + j) % 3
                for half in range(2):
                    bk = gen_bank()
                    for c in range(4):
                        mm(PS[bk], wkv_b[s_][:, c, 0:128], kvnT_all[:, c, half * 4:(half + 1) * 4, j, :], c == 0, c == 3,
                           [b_wkv[s_]] + b_kvall[half * 4:(half + 1) * 4], [PSb[bk]])
                    copy_any(KT[ks][:, half * 4:(half + 1) * 4, :], PS[bk].rearrange("p (r t) -> p r t", r=4), [PSb[bk]],
                             [b_KT[ks]])
                for half in range(2):
                    bk = gen_bank()
                    for rr in range(4):
                        r = half * 4 + rr
                        for c in range(4):
                            mm(PS[bk][:, rr * 128:(rr + 1) * 128], kvnT_all[:, c, r, j, :], wkv_b[s_][:, c, 128:256],
                               c == 0, c == 3, [b_wkv[s_], b_kvall[r]], [PSb[bk]])
                    copy_any(Vb[ks][:, half * 4:(half + 1) * 4, 0:128], PS[bk].rearrange("p (r t) -> p r t", r=4),
                             [PSb[bk]], [b_V[ks]])

            def attn_round(h, j):
                s_ = h % 2
                ks = (h * 8 + j) % 3
                q0 = j * 128
                nq = 1024 - q0
                chunks = [(q0, nq)] if nq <= 512 else [(q0, 512), (q0 + 512, nq - 512)]
                for r in range(8):
                    for (c0, n) in chunks:
                        sb_ = 3 + st_rr[0] % 3
                        pi = st_rr[0] % 3
                        st_rr[0] += 1
                        mm(PS[sb_][:, 0:n], KT[ks][:, r, :], qT[s_][:, c0:c0 + n], True, False,
                           [b_KT[ks], b_qT[s_]], [PSb[sb_]])
                        mm(PS[sb_][:, 0:n], kropeT_all[:, r, j, :], qrT[s_][:, c0:c0 + n], False, True,
                           [b_kvall[r], b_qT[s_]], [PSb[sb_]])
                        act(PT[pi][:, 0:n], PS[sb_][:, 0:n], AF.Exp, [PSb[sb_]], [b_PT[pi]])
                        if c0 == q0:
                            dve("tensor_tensor", [b_PT[pi], b_mask], [b_PT[pi]], out=PT[pi][:, 0:128], in0=PT[pi][:, 0:128],
                                in1=maskb[:, r, :], op=ALU.mult)
                        def pv(r=r, c0=c0, n=n, pi=pi):
                            for qb in range(c0 // 128, (c0 + n) // 128):
                                first = (j == 0 and r == 0 and OC[qb] == 0)
                                last = (j == qb and r == 7)
                                mm(Oap(qb, 0, 129), PT[pi][:, qb * 128 - c0:qb * 128 - c0 + 128], Vb[ks][:, r, 0:129],
                                   first, last, [b_PT[pi], b_V[ks]], [PSb[OB[qb]]], skip=True)
                        pend.append(pv)
                        flush(2)

                def epilogue():
                    qb = j
                    e_ = (h * 8 + j) % 2
                    act(osb[e_][:, 0:129], Oap(qb, 0, 129), AF.Copy, [PSb[OB[qb]]], [b_mtmp[e_]])
                    dve("reciprocal", [b_mtmp[e_]], [b_mtmp[e_]], out=rinv[:, e_:e_ + 1], in_=osb[e_][:, 128:129])
                    dve("scalar_tensor_tensor", [b_mtmp[e_], b_gmh[s_]], [b_mtmp[e_]], out=mtmp[e_],
                        in0=osb[e_][:, 0:128], scalar=rinv[:, e_:e_ + 1], in1=gmh[s_][:, qb, :], op0=ALU.mult,
                        op1=ALU.mult)
                    dve("tensor_tensor", [b_mtmp[e_], b_ysgh[s_]], [b_mtmp[e_]], out=mrgb[e_], in0=mtmp[e_],
                        in1=ysgh[s_][:, qb, :], op=ALU.add)
                    bk = gen_bank()
                    pb = PS[bk].bitcast(BF16)
                    tr(pb[:, 0:128], mrgb[e_], ident_b, [b_mtmp[e_], b_ident_b], [PSb[bk]])
                    act(mergedT[:, h, qb * 128:(qb + 1) * 128], pb[:, 0:128], AF.Copy, [PSb[bk]], [b_mT[qb]])
                pend.append(epilogue)
                if j == 7 and h + 1 < NH:
                    pend.append(lambda: load_gates(h + 1))

            pend = []

            def flush(keep):
                while len(pend) > keep:
                    pend.pop(0)()

            NH = DEBUG.get("nheads", H)
            steps = [(h, j) for h in range(NH) for j in range(8)]
            head_prologue(0)
            gen_round(0, 0)
            for i, (h, j) in enumerate(steps):
                if i + 1 < len(steps):
                    h2, j2 = steps[i + 1]
                    if j2 == 0:
                        head_prologue(h2)
                    gen_round(h2, j2)
                attn_round(h, j)
            flush(0)
            if "mergedT" in dbg_out:
                dma(dbg_out["mergedT"][:, :, :], mergedT, b_mT, [], "dbg")

            if STOP < 8:
                raise _Stop()
            xnew = R1.t[:, 0:16384].rearrange("p (b n) -> p b n", b=NB)
            gate_bc = R1.t[:, 16384:18432]
            fg_bc = R1.t[:, 18432:20480]
            b_xnew = [Buf("xnew%d" % b) for b in range(NB)]
            b_gate, b_fg = Buf("gate"), Buf("fg")
            fence(b_kvall, b_xnew + [b_gate, b_fg])
            dma(gate_bc, mod_d[0:1, 2 * D:3 * D].broadcast_to([128, D]), [b_modd], [b_gate], "c9_9")
            dma(fg_bc, fng_d[0:1, :].broadcast_to([128, D]), [], [b_fg], "c9_10")
            R3.reset()
            w_st = [R3.f32(2 * GW).rearrange("p (k n) -> p k n", k=2) for _ in range(NST)] + w_st_x
            w_bf = [R3.bf16(16 * GW).rearrange("p (k n) -> p k n", k=16) for _ in range(2)]
            fence([b_wqst, b_wkvst] + b_wq + b_wkv + b_qT + b_gmh + b_ysgh + b_KT + b_V,
                  b_wst + [q for l in b_wbf for q in l])
            xr = [R5.t[:, i * 512:(i + 1) * 512] for i in range(3)]
            otmp = [R5.t[:, 1536 + i * 512:1536 + (i + 1) * 512] for i in range(2)]
            b_xr = [Buf("xr%d" % i) for i in range(3)]
            b_otmp = [Buf("otmp%d" % i) for i in range(2)]
            fence(b_qnT, b_xr + b_otmp)
            sq2 = R6.t[:, 0:512]
            b_sq2 = Buf("sq2")
            fence(b_KT + b_V + b_PT + b_mtmp + [b_ropeA], [b_sq2] + b_wst_x)
            cnt = 0
            nxt = load_weight_cg(wout_d, [(0, 512)])
            for n in range(4):
                wt, wb = nxt[0], nxt[1]
                nxt = load_weight_cg(wout_d, [((n + 1) * 512, 512)], lazy=True) if n + 1 < 4 else None
                for b in range(NB):
                    xi = cnt % 3
                    oi = cnt % 2
                    cnt += 1
                    dma(xr[xi], x_d[b * 128:(b + 1) * 128, n * 512:(n + 1) * 512], [], [b_xr[xi]], "xr%d" % xi)
                    bk = next_bank()
                    for k in range(16):
                        mm(PS[bk], mergedT[:, k, b * 128:(b + 1) * 128], wt[:, k, 0:512], k == 0, k == 15,
                           [b_mT[b], wb[k // 2]], [PSb[bk]])
                    dve("tensor_tensor", [PSb[bk], b_gate], [b_otmp[oi]], out=otmp[oi], in0=PS[bk],
                        in1=gate_bc[:, n * 512:(n + 1) * 512], op=ALU.mult)
                    dve("tensor_tensor", [b_otmp[oi], b_xr[xi]], [b_xnew[b]], out=xnew[:, b, n * 512:(n + 1) * 512],
                        in0=otmp[oi], in1=xr[xi], op=ALU.add)
                    act(sq2, xnew[:, b, n * 512:(n + 1) * 512], AF.Square, [b_xnew[b]], [b_sq2, b_stf[b]],
                        accum_out=st_f4[:, b, n:n + 1])
                    if nxt is not None:
                        nxt[2](b)
            for b in range(NB):
                dve("tensor_reduce", [b_stf[b]], [b_stf[b]], out=st_f[:, b:b + 1], in_=st_f4[:, b, :], axis=AX.X, op=ALU.add)
                rstd_from_ssq(st_f[:, b:b + 1], b_stf[b], D)
                dve("scalar_tensor_tensor", [b_xnew[b], b_stf[b], b_fg], [b_xnew[b]], out=xnew[:, b, :], in0=xnew[:, b, :],
                    scalar=st_f[:, b:b + 1], in1=fg_bc, op0=ALU.mult, op1=ALU.mult)
                dma(out_d[b * 128:(b + 1) * 128, :], xnew[:, b, :], [b_xnew[b]], [], "out%d" % (b % 2))
        except _Stop:
            pass
        P.emit(es)
    return nc, P


def _prep_inputs(inputs):
    f = lambda k: np.ascontiguousarray(np.asarray(inputs[k]), dtype=np.float32)
    x = f("x").reshape(S // 128, 128, D)
    pos = np.ascontiguousarray(np.asarray(inputs["positions"]), dtype=np.int32).reshape(S // 128, 128)
    c = f("c").reshape(16, 128).T.copy()
    w_uq = f("w_uq").reshape(QL, H, 192)
    nope = w_uq[:, :, 0:128]
    rope = w_uq[:, :, 128:192]
    ropesw = np.concatenate([w_uq[:, :, 160:192], w_uq[:, :, 128:160]], axis=2)
    wuq = np.ascontiguousarray(np.concatenate([nope, rope, rope, ropesw, ropesw], axis=2).transpose(1, 0, 2))
    wukv = np.ascontiguousarray(f("w_ukv").reshape(KVL, H, 256).transpose(1, 0, 2))
    wsT = np.ascontiguousarray(f("w_spatial").reshape(16, 128, 128).transpose(2, 0, 1))
    bspT = np.ascontiguousarray(f("b_spatial").reshape(16, 128).T)
    tri = np.triu(np.ones((128, 128), np.float32))
    ident = np.eye(128, dtype=np.float32)
    invf = (1.0 / (10000.0 ** (np.arange(0, 64, 2, dtype=np.float32) / np.float32(64)))).astype(np.float32)
    invf_bc = np.ascontiguousarray(np.broadcast_to(invf[None, :], (128, 32)))
    w_in = f("w_in").reshape(D, INW)
    cols = list(range(0, 1344)) + list(range(OFF_V, OFF_V + D))
    for g in range(H):
        for off in (OFF_U, OFF_ZS, OFF_GS, OFF_ZM, OFF_GM):
            cols += list(range(off + g * 128, off + (g + 1) * 128))
    w_in_p = np.ascontiguousarray(w_in[:, np.asarray(cols)])
    shared = {
        "cT": c, "w_ada": f("w_ada").reshape(D, 3 * D), "b_ada": f("b_ada").reshape(1, 3 * D),
        "attn_norm_g": f("attn_norm_g").reshape(1, D), "w_in": w_in_p,
        "q_norm_g": f("q_norm_g").reshape(1, QL), "kv_norm_g": f("kv_norm_g").reshape(1, KVL),
        "w_uq": wuq, "w_ukv": wukv, "sgu_norm_g": f("sgu_norm_g").reshape(1, D), "wsT": wsT, "bspT": bspT,
        "w_out": f("w_out").reshape(D, D), "final_norm_g": f("final_norm_g").reshape(1, D),
        "tri": tri, "ident": ident, "invf": invf_bc,
    }
    in_maps = []
    for i in range(NCORES):
        blocks = [8 * j + i for j in range(NB)]
        m = dict(shared)
        m["x"] = np.ascontiguousarray(x[blocks].reshape(TPC, D))
        m["pos"] = np.ascontiguousarray(pos[blocks].T)
        mask = np.zeros((128, 8, 128), np.float32)
        for r in range(8):
            if r < i:
                mask[:, r, :] = 1.0
            elif r == i:
                mask[:, r, :] = tri
        m["mask"] = mask
        in_maps.append(m)
    return in_maps


def kernel(**inputs):
    in_maps = _prep_inputs(inputs)
    dbg = DEBUG.get("dbg", ())
    nc, P = build_program(dbg)
    res = run_bass_kernel_spmd(nc, in_maps, core_ids=list(range(NCORES)))
    out = np.empty((S // 128, 128, D), np.float32)
    for i in range(NCORES):
        oc = np.asarray(res.results[i]["out"]).reshape(NB, 128, D)
        for j in range(NB):
            out[8 * j + i] = oc[j]
    if dbg:
        DEBUG["results"] = res.results
    return out.reshape(1, S, D)
```
